# Optimizing a Trainium2 kernel written in Bass

```python
import jax, jax.numpy as jnp
from jax import lax
import numpy as np

D_MODEL = 2048
BATCH = 4
SEQ = 2048
DEPTH = 1
DEC_BATCH = 2
DEC_SEQ = 4096
PAST_LEN = 128

HEAD_DIM = 128
ATTN_GROUPS = ((128, 1), (512, 4), (2048, 16))
N_GROUPS = 3
HEADS_PER_GROUP = D_MODEL // 256
ATTN_QKV = N_GROUPS * HEADS_PER_GROUP * HEAD_DIM
ATTN_OUT = HEADS_PER_GROUP * HEAD_DIM
ROT_DIM = HEAD_DIM // 4
ROPE_THETA = 500000.0
GLA_HEADS = 4
GLA_KEY = D_MODEL // 2
GLA_VAL = D_MODEL
GLA_DK = GLA_KEY // GLA_HEADS
GLA_DV = GLA_VAL // GLA_HEADS
GLA_RANK = 16
GLA_NORMALIZER = 16.0
GLA_CHUNK = 64
D_FF = 4 * D_MODEL
EPS = 1e-6
IN_SPLITS = (ATTN_QKV, ATTN_QKV, ATTN_QKV, GLA_KEY, GLA_KEY, GLA_VAL, GLA_VAL, GLA_RANK, GLA_RANK, D_MODEL, D_MODEL)
IN_COLS = 3 * ATTN_QKV + 2 * GLA_KEY + 2 * GLA_VAL + 2 * GLA_RANK + 2 * D_MODEL

kernel_name = "hybrid_dilated_attn_gla_encoder"


def rmsnorm(x, gain):
    x32 = x.astype(jnp.float32)
    y = x32 * lax.rsqrt(jnp.mean(x32 * x32, axis=-1, keepdims=True) + EPS)
    return (y * gain.astype(jnp.float32)).astype(x.dtype)


def partial_rope(x, pos):
    half = ROT_DIM // 2
    inv_freq = ROPE_THETA ** (-jnp.arange(0, ROT_DIM, 2, dtype=jnp.float32) / ROT_DIM)
    ang = pos[:, None] * inv_freq[None, :]
    ang = ang.reshape((ang.shape[0],) + (1,) * (x.ndim - 3) + (half,))
    cos, sin = jnp.cos(ang), jnp.sin(ang)
    x32 = x.astype(jnp.float32)
    x1, x2 = x32[..., :half], x32[..., half:ROT_DIM]
    out = jnp.concatenate([x1 * cos - x2 * sin, x2 * cos + x1 * sin, x32[..., ROT_DIM:]], axis=-1)
    return out.astype(x.dtype)


def dilated_window_attention(q, k, v, dilation, half):
    B, S, H, Dh = q.shape
    r = dilation
    L = S // r
    nb = -(-L // half)
    Lp = nb * half

    def to_sub(t):
        return t.astype(jnp.float32).reshape(B, L, r, H, Dh).transpose(0, 2, 1, 3, 4)

    qs, ks, vs = to_sub(q), to_sub(k), to_sub(v)
    qs = jnp.pad(qs, ((0, 0), (0, 0), (0, Lp - L), (0, 0), (0, 0))).reshape(B, r, nb, half, H, Dh)
    kpad = ((0, 0), (0, 0), (half, Lp - L + half), (0, 0), (0, 0))
    kblk = jnp.pad(ks, kpad).reshape(B, r, nb + 2, half, H, Dh)
    vblk = jnp.pad(vs, kpad).reshape(B, r, nb + 2, half, H, Dh)
    kb = jnp.concatenate([kblk[:, :, :-2], kblk[:, :, 1:-1], kblk[:, :, 2:]], axis=3)
    vb = jnp.concatenate([vblk[:, :, :-2], vblk[:, :, 1:-1], vblk[:, :, 2:]], axis=3)

    a = jnp.arange(half)[:, None]
    b = jnp.arange(3 * half)[None, :]
    rel = b - half - a
    kpos = jnp.arange(nb)[:, None, None] * half - half + b[None]
    mask = (jnp.abs(rel) <= half)[None] & (kpos >= 0) & (kpos < L)

    s = jnp.einsum('brnqhd,brnkhd->brnhqk', qs, kb) * (Dh ** -0.5)
    s = jnp.where(mask[None, None, :, None], s, -jnp.inf)
    m = jnp.max(s, axis=-1, keepdims=True)
    p = jnp.exp(s - m)
    den = jnp.sum(p, axis=-1)
    o = jnp.einsum('brnhqk,brnkhd->brnqhd', p, vb) / den.transpose(0, 1, 2, 4, 3)[..., None]
    lse = (m[..., 0] + jnp.log(den)).transpose(0, 1, 2, 4, 3)
    o = o.reshape(B, r, Lp, H, Dh)[:, :, :L].transpose(0, 2, 1, 3, 4).reshape(B, S, H, Dh)
    lse = lse.reshape(B, r, Lp, H)[:, :, :L].transpose(0, 2, 1, 3).reshape(B, S, H)
    return o, lse


def gla_direction(q, k, v, log_a, strict):
    B, S, H, DK = q.shape
    DV = v.shape[-1]
    C = GLA_CHUNK
    N = S // C

    def chunks(t):
        return t.astype(jnp.float32).reshape(B, N, C, H, t.shape[-1]).transpose(1, 0, 3, 2, 4)

    qc, kc, vc, gc = chunks(q), chunks(k), chunks(v), chunks(log_a)
    tri = jnp.tril(jnp.ones((C, C), dtype=bool), -1 if strict else 0)

    def step(state, inp):
        qi, ki, vi, gi = inp
        bcum = jnp.cumsum(gi, axis=2)
        blast = bcum[:, :, -1:, :]
        o_inter = jnp.einsum('bhck,bhkv->bhcv', qi * jnp.exp(bcum), state)
        diff = bcum[:, :, :, None, :] - bcum[:, :, None, :, :]
        decay = jnp.exp(jnp.where(tri[:, :, None], diff, -jnp.inf))
        att = jnp.einsum('bhik,bhjk,bhijk->bhij', qi, ki, decay)
        o_intra = jnp.einsum('bhij,bhjv->bhiv', att, vi)
        new_state = jnp.exp(blast)[:, :, 0, :, None] * state + jnp.einsum(
            'bhck,bhcv->bhkv', ki * jnp.exp(blast - bcum), vi)
        return new_state, o_inter + o_intra

    init = jnp.zeros((B, H, DK, DV), jnp.float32)
    _, out = lax.scan(step, init, (qc, kc, vc, gc))
    return out.transpose(1, 0, 3, 2, 4).reshape(B, S, H, DV)


def encoder_layer(x, norm_mix, w_in, q_norm, k_norm, w_gla_gate, b_gla_gate, gla_norm,
                  w_branch_attn, w_branch_gla, w_out, norm_ffn, w_ff1, w_ff2):
    B, S, _ = x.shape
    xn = rmsnorm(x, norm_mix)
    proj = xn @ w_in
    cuts = [int(c) for c in np.cumsum(IN_SPLITS)[:-1]]
    qa, ka, va, qg, kg, vg, rg, lrf, lrb, ga, gb = jnp.split(proj, cuts, axis=-1)

    def heads(t):
        return t.reshape(B, S, N_GROUPS, HEADS_PER_GROUP, HEAD_DIM)
    pos = jnp.arange(S, dtype=jnp.float32)
    qa = partial_rope(rmsnorm(heads(qa), q_norm[:, None, :]), pos)
    ka = partial_rope(rmsnorm(heads(ka), k_norm[:, None, :]), pos)
    va = heads(va)
    outs, lses = [], []
    for g, (window, dil) in enumerate(ATTN_GROUPS):
        o_g, l_g = dilated_window_attention(qa[:, :, g], ka[:, :, g], va[:, :, g], dil, window // (2 * dil))
        outs.append(o_g)
        lses.append(l_g)
    wts = jax.nn.softmax(jnp.stack(lses), axis=0)
    o_attn = jnp.einsum('gbsh,gbshd->bshd', wts, jnp.stack(outs)).reshape(B, S, ATTN_OUT)

    qg = qg.reshape(B, S, GLA_HEADS, GLA_DK) * (GLA_DK ** -0.5)
    kg = kg.reshape(B, S, GLA_HEADS, GLA_DK)
    vg = vg.reshape(B, S, GLA_HEADS, GLA_DV)
    def log_gate(lr, d):
        z = lr.astype(jnp.float32) @ w_gla_gate[d].astype(jnp.float32) + b_gla_gate[d].astype(jnp.float32)
        return (jax.nn.log_sigmoid(z) / GLA_NORMALIZER).reshape(B, S, GLA_HEADS, GLA_DK)
    o_fwd = gla_direction(qg, kg, vg, log_gate(lrf, 0), False)
    flip = lambda t: jnp.flip(t, axis=1)
    o_bwd = flip(gla_direction(flip(qg), flip(kg), flip(vg), flip(log_gate(lrb, 1)), True))
    o_gla = rmsnorm(o_fwd + o_bwd, gla_norm) * jax.nn.silu(rg.reshape(B, S, GLA_HEADS, GLA_DV).astype(jnp.float32))
    o_gla = o_gla.reshape(B, S, GLA_VAL)

    u_a = o_attn.astype(x.dtype) @ w_branch_attn
    u_b = o_gla.astype(x.dtype) @ w_branch_gla
    merged = jax.nn.sigmoid(ga) * u_a + jax.nn.sigmoid(gb) * u_b
    h = x + merged @ w_out

    hn = rmsnorm(h, norm_ffn)
    return h + jnp.square(jax.nn.relu(hn @ w_ff1)) @ w_ff2


def trunk(x, norm_mix, w_in, q_norm, k_norm, w_gla_gate, b_gla_gate, gla_norm,
          w_branch_attn, w_branch_gla, w_out, norm_ffn, w_ff1, w_ff2):
    for layer in range(DEPTH):
        x = encoder_layer(x, norm_mix[layer], w_in[layer], q_norm[layer], k_norm[layer],
                          w_gla_gate[layer], b_gla_gate[layer], gla_norm[layer],
                          w_branch_attn[layer], w_branch_gla[layer], w_out[layer],
                          norm_ffn[layer], w_ff1[layer], w_ff2[layer])
    return x


def setup_inputs(seed: int = 0) -> dict:
    key = jax.random.key(seed)
    ks = jax.random.split(key, 16)
    f32 = jnp.float32
    def nrm(k, shape, scale):
        return jax.random.normal(k, shape, f32) * scale
    def gain(k, shape):
        return 1.0 + 0.02 * jax.random.normal(k, shape, f32)
    return {
        'x_prompt': nrm(ks[0], (BATCH, SEQ, D_MODEL), 1.0),
        'x_sample': nrm(ks[1], (DEC_BATCH, DEC_SEQ, D_MODEL), 1.0),
        'norm_mix': gain(ks[2], (DEPTH, D_MODEL)),
        'w_in': nrm(ks[3], (DEPTH, D_MODEL, IN_COLS), D_MODEL ** -0.5),
        'q_norm': gain(ks[4], (DEPTH, N_GROUPS, HEAD_DIM)),
        'k_norm': gain(ks[5], (DEPTH, N_GROUPS, HEAD_DIM)),
        'w_gla_gate': nrm(ks[6], (DEPTH, 2, GLA_RANK, GLA_KEY), GLA_RANK ** -0.5),
        'b_gla_gate': nrm(ks[7], (DEPTH, 2, GLA_KEY), 0.1),
        'gla_norm': gain(ks[8], (DEPTH, GLA_DV)),
        'w_branch_attn': nrm(ks[9], (DEPTH, ATTN_OUT, D_MODEL), ATTN_OUT ** -0.5),
        'w_branch_gla': nrm(ks[10], (DEPTH, GLA_VAL, D_MODEL), GLA_VAL ** -0.5),
        'w_out': nrm(ks[11], (DEPTH, D_MODEL, D_MODEL), D_MODEL ** -0.5),
        'norm_ffn': gain(ks[12], (DEPTH, D_MODEL)),
        'w_ff1': nrm(ks[13], (DEPTH, D_MODEL, D_FF), D_MODEL ** -0.5),
        'w_ff2': nrm(ks[14], (DEPTH, D_FF, D_MODEL), D_FF ** -0.5),
    }


def reference(x_prompt, x_sample, norm_mix, w_in, q_norm, k_norm, w_gla_gate, b_gla_gate, gla_norm,
              w_branch_attn, w_branch_gla, w_out, norm_ffn, w_ff1, w_ff2):
    y_prompt = trunk(x_prompt, norm_mix, w_in, q_norm, k_norm, w_gla_gate, b_gla_gate, gla_norm,
                     w_branch_attn, w_branch_gla, w_out, norm_ffn, w_ff1, w_ff2)
    y_sample = trunk(x_sample, norm_mix, w_in, q_norm, k_norm, w_gla_gate, b_gla_gate, gla_norm,
                     w_branch_attn, w_branch_gla, w_out, norm_ffn, w_ff1, w_ff2)
    return (y_prompt, y_sample)
```

```python
import numpy as np
from contextlib import ExitStack
import concourse.bass as bass
import concourse.mybir as mybir
from concourse.bass_utils import run_bass_kernel_spmd

F32 = mybir.dt.float32
BF16 = mybir.dt.bfloat16
AF = mybir.ActivationFunctionType
ALU = mybir.AluOpType

D = 2048
T = 2048
NCORES = 8
INC = 19488
C_QA, C_KA, C_VA = 0, 3072, 6144
C_QG, C_KG, C_VG, C_RG = 9216, 10240, 11264, 13312
C_LRF, C_LRB, C_GA, C_GB = 15360, 15376, 15392, 17440
DIL = (1, 4, 16)
EPS = 1e-6
NSLOT = 3


class Buf:
    __slots__ = ("name", "w", "r")

    def __init__(self, name):
        self.name = name
        self.w = []
        self.r = {}


class TB:
    def __init__(self, t, name):
        self.t = t
        self.b = Buf(name)


class DSem:
    def __init__(self, sem, key):
        self.sem = sem
        self.key = key
        self.count = 0


class KB:
    def __init__(self, nc, es):
        self.nc = nc
        self.engs = {"pe": nc.tensor, "act": nc.scalar, "dve": nc.vector, "pool": nc.gpsimd, "sp": nc.sync}
        self.sem = {k: es.enter_context(nc.semaphore("sem_" + k)) for k in self.engs}
        self.cnt = {k: 0 for k in self.engs}
        self.seen = {k: {} for k in self.engs}
        self.dsems = {"sp": [DSem(es.enter_context(nc.semaphore(f"dsp{i}")), f"dsp{i}") for i in range(8)],
                      "pool": [DSem(es.enter_context(nc.semaphore(f"dpl{i}")), f"dpl{i}") for i in range(8)]}
        self.didx = {"sp": 0, "pool": 0, "pc": 0}
        self.dsems["pc"] = [DSem(es.enter_context(nc.semaphore(f"dpc{i}")), f"dpc{i}") for i in range(4)]
        self.engs["pc"] = nc.gpsimd
        self.seen["pc"] = self.seen["pool"]
        self.psb = [TB(es.enter_context(nc.psum_tensor(f"psb{i}", [128, 512], F32)), f"psb{i}") for i in range(8)]
        self.psfree = list(range(8))
        self.nins = 0
        self.mute = False

    def sb(self, name, shape, dt, es):
        self.nsb = getattr(self, "nsb", 0) + 1
        return TB(es.enter_context(self.nc.sbuf_tensor(f"s{self.nsb}_{name}", list(shape), dt)), name)

    def ps(self):
        assert self.psfree, "out of PSUM banks"
        i = self.psfree.pop(0)
        p = self.psb[i]
        p.idx = i
        return p

    def psf(self, *ps):
        for p in ps:
            self.psfree.append(p.idx)

    def _wait(self, eng, deps):
        if self.mute:
            return
        best = {}
        for (s, key, v) in deps:
            if key == "pe" and eng == "pe":
                continue
            if key not in best or v > best[key][1]:
                best[key] = (s, v)
        for key, (s, v) in best.items():
            if self.seen[eng].get(key, 0) < v:
                self.engs[eng].wait_ge(s, v)
                self.seen[eng][key] = v

    @staticmethod
    def _deps(R, W):
        deps = []
        for b in R:
            deps.extend(b.w)
        for b in W:
            deps.extend(b.w)
            deps.extend(b.r.values())
        return deps

    @staticmethod
    def _record(toks, R, W):
        for b in R:
            for tok in toks:
                old = b.r.get(tok[1])
                if old is None or old[2] < tok[2]:
                    b.r[tok[1]] = tok
        for b in W:
            b.w = list(toks)
            b.r = {}

    def op(self, eng, fn, R=(), W=()):
        if self.mute:
            return None
        self._wait(eng, self._deps(R, W))
        ins = fn()
        self.cnt[eng] += 1
        ins.then_inc(self.sem[eng], 1)
        tok = (self.sem[eng], eng, self.cnt[eng])
        self._record([tok], R, W)
        self.nins += 1
        return tok

    def dma(self, q, pairs, R=(), W=()):
        if self.mute:
            return []
        self._wait(q, self._deps(R, W))
        toks = []
        for (o, i) in pairs:
            ds = self.dsems[q][self.didx[q]]
            self.didx[q] = (self.didx[q] + 1) % len(self.dsems[q])
            if ds.count > 0 and self.seen[q].get(ds.key, 0) < ds.count:
                self.engs[q].wait_ge(ds.sem, ds.count)
                self.seen[q][ds.key] = ds.count
            ins = self.engs[q].dma_start(out=o, in_=i)
            ds.count += 16
            ins.then_inc(ds.sem, 16)
            toks.append((ds.sem, ds.key, ds.count))
            self.nins += 1
        self._record(toks, R, W)
        return toks

    def all_tokens(self):
        toks = [(self.sem[e], e, self.cnt[e]) for e in self.cnt if self.cnt[e] > 0]
        for q in self.dsems:
            toks += [(s.sem, s.key, s.count) for s in self.dsems[q] if s.count > 0]
        return toks

    def barrier(self):
        toks = self.all_tokens()
        for e in self.engs:
            self._wait(e, [t for t in toks if t[1] != e])

    def act(self, out, in_, func, R, W, bias=None, scale=None, accum=None):
        def fn():
            kw = {}
            if bias is not None:
                kw["bias"] = bias
            if scale is not None:
                kw["scale"] = scale
            if accum is not None:
                kw["accum_out"] = accum
            return self.nc.scalar.activation(out=out, in_=in_, func=func, **kw)
        return self.op("act", fn, R, W)

    def tt(self, out, in0, in1, op, R, W, eng="dve"):
        e = self.engs[eng]
        return self.op(eng, lambda: e.tensor_tensor(out=out, in0=in0, in1=in1, op=op), R, W)

    def ts(self, out, in0, s1, s2, op0, op1, R, W, eng="dve"):
        e = self.engs[eng]
        if op1 is None:
            return self.op(eng, lambda: e.tensor_scalar(out=out, in0=in0, scalar1=s1, scalar2=None, op0=op0), R, W)
        return self.op(eng, lambda: e.tensor_scalar(out=out, in0=in0, scalar1=s1, scalar2=s2, op0=op0, op1=op1), R, W)

    def stt(self, out, in0, scalar, in1, op0, op1, R, W):
        return self.op("dve", lambda: self.nc.vector.scalar_tensor_tensor(out=out, in0=in0, scalar=scalar, in1=in1,
                                                                           op0=op0, op1=op1), R, W)

    def recip(self, out, in_, R, W):
        return self.op("dve", lambda: self.nc.vector.reciprocal(out=out, in_=in_), R, W)

    def copy(self, eng, out, in_, R, W):
        if eng == "act":
            return self.op("act", lambda: self.nc.scalar.activation(out=out, in_=in_, func=AF.Copy), R, W)
        e = self.engs[eng]
        return self.op(eng, lambda: e.tensor_copy(out=out, in_=in_), R, W)

    def memset(self, eng, ap, val, W):
        e = self.engs[eng]
        return self.op(eng, lambda: e.memset(ap, val), [], W)

    def mm(self, items, R, W):
        def fn():
            ins = None
            for (o, l, r, st, sp) in items:
                ins = self.nc.tensor.matmul(o, lhsT=l, rhs=r, start=st, stop=sp)
            return ins
        return self.op("pe", fn, R, W)

    def tr(self, items, ident, R, W):
        def fn():
            ins = None
            for (o, i) in items:
                ins = self.nc.tensor.transpose(o, i, ident.t[:])
            return ins
        return self.op("pe", fn, list(R) + [ident.b], W)


class WRing:
    def __init__(self, kb, es, dram, sched):
        self.kb = kb
        self.d = dram
        self.slots = [kb.sb(f"wslot{i}", [128, 16, 512], BF16, es) for i in range(NSLOT)]
        for i, s in enumerate(self.slots):
            s.idx = i
        self.sched = sched
        self.rec = []
        self.free = list(range(NSLOT))
        self.slot_of = {}
        self.nfetch = 0
        self.nacq = 0

    def _fetch(self, i, spec):
        s = self.free.pop(0)
        self.slot_of[i] = s
        slot = self.slots[s]
        if spec[0][0].startswith("@"):
            (name, tile, nel) = spec[0]
            flat = slot.t[:].rearrange("p k c -> p (k c)")
            self.kb.dma("sp", [(flat[:, 0:nel], self.d[name[1:]][tile].rearrange("p k c -> p (k c)"))], R=[], W=[slot.b])
            return
        pairs = []
        for (name, r0, nr, c0, ncol, dc0) in spec:
            kc = nr // 128
            src = self.d[name][r0:r0 + nr, c0:c0 + ncol].rearrange("(k p) c -> p k c", p=128)
            pairs.append((slot.t[:, 0:kc, dc0:dc0 + ncol], src))
        self.kb.dma("pool", pairs, R=[], W=[slot.b])

    def prefetch(self):
        if self.sched is None or self.kb.mute:
            return
        while self.free and self.nfetch < len(self.sched):
            self._fetch(self.nfetch, self.sched[self.nfetch])
            self.nfetch += 1

    def acquire(self, spec):
        if self.kb.mute:
            return self.slots[0]
        spec = tuple(spec)
        i = self.nacq
        self.nacq += 1
        self.rec.append(spec)
        if self.sched is not None:
            assert self.sched[i] == spec, (i, self.sched[i], spec)
        if i >= self.nfetch:
            assert self.free, "weight ring full"
            self._fetch(i, spec)
            self.nfetch = i + 1
        return self.slots[self.slot_of[i]]

    def release(self, slot):
        if self.kb.mute:
            return
        self.free.append(slot.idx)
        self.prefetch()


class StopBuild(Exception):
    pass


STOP = [0]


class Builder:
    def ck(self, n):
        if STOP[0] == n:
            self.kb.barrier()
            self.kb.mute = True

    def __init__(self, dbg, sched):
        self.dbg = dbg
        self.nc = nc = bass.Bass("TRN2", target_bir_lowering=False)
        self.d = d = {}

        def din(name, shape):
            d[name] = nc.dram_tensor(name, list(shape), F32, kind="ExternalInput").ap()

        din("x_own", [T, D]); din("x_ctx", [T, D]); din("w_in", [D, INC]); din("w_lr", [D, 32])
        din("w_gate", [16, 2, 1024]); din("b_gate", [128, 16])
        din("norm_mix", [D]); din("norm_ffn", [D]); din("gla_norm", [512]); din("qk_gain", [128, 6])
        din("w_ba", [1024, D]); din("w_bg", [D, D]); din("w_out", [D, D]); din("w_ff1", [D, 8192]); din("w_ff2", [8192, D])
        din("cos_t", [32, 3072]); din("sin_t", [32, 3072]); din("amask", [128, 4, 256]); din("ident", [128, 128])
        din("perm", [32, 32]); din("trim", [128, 2, 128]); din("rmask", [128, 512])
        d["y"] = nc.dram_tensor("y", [T, D], F32, kind="ExternalOutput").ap()
        sk = "ExternalOutput" if dbg else "Internal"

        def dsc(name, shape, dt):
            d[name] = nc.dram_tensor(name, list(shape), dt, kind=sk).ap()

        dsc("oaT_d", [8, 128, T], BF16); dsc("ogT_d", [16, 128, T], BF16); dsc("mT_d", [16, 128, T], BF16)
        dsc("hnT_d", [16, 128, T], BF16); dsc("h_d", [T, D], F32); dsc("ob_d", [4, T, 512], F32)
        dsc("vsc_d", [4, 128, 16, 512], BF16); dsc("sctx_d", [8, 128, 512], F32); dsc("kh_d", [24, 128, 1024], BF16); dsc("vh_d", [24, 128, 1024], BF16)
        for nm, shp in (("c_m", [16, 128, 56, 128]), ("c_o", [4, 128, 16, 512]), ("c_f1", [16, 128, 16, 512]), ("c_f2", [16, 128, 16, 512])):
            d[nm] = nc.dram_tensor(nm, shp, BF16, kind="Internal").ap()
        jobs = []
        for fc in range(16):
            c0 = fc * 128
            jobs += [("c_m", fc, 0, "w_ba", 0, 1024, c0, 128), ("c_m", fc, 8, "w_bg", 0, D, c0, 128),
                     ("c_m", fc, 24, "w_in", 0, D, C_GA + c0, 128), ("c_m", fc, 40, "w_in", 0, D, C_GB + c0, 128)]
        for cb in range(4):
            jobs.append(("c_o", cb, 0, "w_out", 0, D, cb * 512, 512))
        for fg in range(16):
            jobs.append(("c_f1", fg, 0, "w_ff1", 0, D, fg * 512, 512))
        for cb in range(4):
            for kg in range(4):
                jobs.append(("c_f2", cb * 4 + kg, 0, "w_ff2", kg * 2048, 2048, cb * 512, 512))
        self.jobs = jobs
        self.njob = 0
        self.sched = sched

    def build(self):
        nc = self.nc
        with ExitStack() as es:
            self.kb = kb = KB(nc, es)
            self.W = WRing(kb, es, self.d, self.sched)
            self.consts(es)
            self.body()
            kb.mute = False
            kb._wait("sp", kb.all_tokens())
        return nc

    def body(self):
        kb = self.kb
        if True:
            self.ck(1)
            self.W.prefetch()
            with ExitStack() as es_c:
                xc = [kb.sb(f"xc{i}", [128, 16, 1024], BF16, es_c) for i in range(2)]
                with ExitStack() as es_s:
                    self.S = [[kb.sb(f"S{h}_{c}", [128, 512], F32, es_s) for c in range(2)] for h in range(4)]
                    for h in range(4):
                        for c in range(2):
                            kb.memset("dve", self.S[h][c].t[:], 0.0, [self.S[h][c].b])
                    for half in range(2):
                        self.norm_transpose(self.d["x_ctx"][half * 1024:(half + 1) * 1024, :], 8, xc[half].t, "norm_mix")
                        self.ck(2)
                        with ExitStack() as es_g:
                            g = self.gla_alloc(es_g, 1024, False)
                            self.lr_project(xc[half], 1024, g, (0,))
                            for h in range(4):
                                self.gla_sweep(h, xc[half], 1024, False, 0, "state", self.S[h], g)
                            kb.barrier()
                        self.ck(3)
                    for h in range(4):
                        for c in range(2):
                            kb.dma("sp", [(self.d["sctx_d"][h * 2 + c], self.S[h][c].t[:])], R=[self.S[h][c].b], W=[])
                    kb.barrier()
                    self.ck(4)
                with ExitStack() as es_a:
                    a = self.attn_alloc(es_a, halo_only=True)
                    for h in range(8):
                        for gi in range(3):
                            self.attn_halo(h, gi, xc[1], a)
                            if h == 0:
                                self.ck(41 + gi)
                    kb.barrier()
            self.ck(5)
            with ExitStack() as es_x:
                self.xn = kb.sb("xnT", [128, 16, T], BF16, es_x)
                self.norm_transpose(self.d["x_own"], 16, self.xn.t, "norm_mix")
                self.ck(6)
                with ExitStack() as es_a:
                    a = self.attn_alloc(es_a, halo_only=False)
                    for h in range(8):
                        for gi in range(3):
                            self.precache(3)
                            self.attn_head(h, gi, a)
                        self.attn_finish(h, a)
                    kb.barrier()
                self.ck(7)
                with ExitStack() as es_g:
                    g = self.gla_alloc(es_g, T, True)
                    self.lr_project(self.xn, T, g, (0, 1))
                    for h in range(4):
                        S = g["S"]
                        for c in range(2):
                            kb.dma("sp", [(S[c].t[:], self.d["sctx_d"][h * 2 + c])], R=[], W=[S[c].b])
                        self.precache(4)
                        self.obw = []
                        self.vsw = []
                        self.gla_sweep(h, self.xn, T, False, 0, "first", S, g)
                        self.precache(4)
                        for c in range(2):
                            kb.memset("dve", S[c].t[:], 0.0, [S[c].b])
                        self.gla_sweep(h, self.xn, T, True, 1, "final", S, g)
                    kb.barrier()
                self.precache(1000)
                kb.barrier()
                self.ck(8)
                self.merge_stage()
            self.ck(9)
            self.out_stage()
            self.ck(10)
            self.ffn_stage()

    def precache(self, n):
        for _ in range(n):
            if self.njob >= len(self.jobs):
                return
            (cn, tile, k0, sn, r0, nr, c0, ncol) = self.jobs[self.njob]
            self.njob += 1
            kc = nr // 128
            src = self.d[sn][r0:r0 + nr, c0:c0 + ncol].rearrange("(k p) c -> p k c", p=128)
            self.kb.dma("pc", [(self.d[cn][tile, :, k0:k0 + kc, :], src)])

    def consts(self, es):
        kb, d = self.kb, self.d
        self.ident = kb.sb("ident", [128, 128], BF16, es)
        kb.dma("pool", [(self.ident.t[:], d["ident"][:, :])], W=[self.ident.b])
        self.amask = kb.sb("amask", [128, 4, 256], BF16, es)
        kb.dma("pool", [(self.amask.t[:], d["amask"][:, :, :])], W=[self.amask.b])
        self.onesf = kb.sb("onesf", [128, 128], F32, es)
        kb.memset("dve", self.onesf.t[:], 1.0, [self.onesf.b])
        self.onesb = kb.sb("onesb", [128, 128], BF16, es)
        kb.memset("dve", self.onesb.t[:], 1.0, [self.onesb.b])
        self.perm = kb.sb("perm", [32, 32], F32, es)
        kb.dma("sp", [(self.perm.t[:], d["perm"][:, :])], W=[self.perm.b])
        self.permb = kb.sb("permb", [32, 32], BF16, es)
        kb.dma("pool", [(self.permb.t[:], d["perm"][:, :])], W=[self.permb.b])
        self.trim = kb.sb("trim", [128, 2, 128], F32, es)
        kb.dma("sp", [(self.trim.t[:], d["trim"][:, :, :])], W=[self.trim.b])
        self.rmask = kb.sb("rmask", [128, 512], F32, es)
        kb.dma("sp", [(self.rmask.t[:], d["rmask"][:, :])], W=[self.rmask.b])
        self.qkg = kb.sb("qkg", [128, 6], F32, es)
        kb.dma("sp", [(self.qkg.t[:], d["qk_gain"][:, :])], W=[self.qkg.b])
        self.wgate = kb.sb("wgate", [16, 2, 1024], F32, es)
        kb.dma("sp", [(self.wgate.t[:], d["w_gate"][:, :, :])], W=[self.wgate.b])
        self.nbg = kb.sb("nbg", [128, 16], F32, es)
        kb.dma("sp", [(self.nbg.t[:], d["b_gate"][:, :])], W=[self.nbg.b])
        kb.ts(self.nbg.t[:], self.nbg.t[:], -1.0, None, ALU.mult, None, R=[self.nbg.b], W=[self.nbg.b])
        self.cc = kb.sb("cc", [128, 4], F32, es)
        kb.memset("dve", self.cc.t[:, 0:1], EPS, [self.cc.b])
        kb.memset("dve", self.cc.t[:, 1:2], 1.0, [self.cc.b])
        kb.memset("dve", self.cc.t[:, 2:3], float(np.log(1.0 / 16.0)), [self.cc.b])
        kb.memset("dve", self.cc.t[:, 3:4], 0.0, [self.cc.b])
        kb.barrier()

    def rstd_from_ss(self, s, n):
        kb = self.kb
        kb.ts(s.t[:, 1:2], s.t[:, 0:1], 1.0 / n, EPS, ALU.mult, ALU.add, R=[s.b], W=[s.b])
        kb.act(s.t[:, 2:3], s.t[:, 1:2], AF.Sqrt, R=[s.b], W=[s.b])
        kb.recip(s.t[:, 3:4], s.t[:, 2:3], R=[s.b], W=[s.b])

    def norm_transpose(self, xd, ntiles, dst, gain_name, src_tiles=None):
        kb = self.kb
        with ExitStack() as es:
            xs = [kb.sb(f"n_x{i}", [128, D], F32, es) for i in range(2)]
            junk = kb.sb("n_junk", [128, D], BF16, es)
            xnb = [kb.sb(f"n_xnb{i}", [128, D], BF16, es) for i in range(2)]
            gbc = kb.sb("n_gbc", [128, D], F32, es)
            sm = [kb.sb(f"n_sm{i}", [128, 4], F32, es) for i in range(2)]
            kb.dma("sp", [(gbc.t[:], self.d[gain_name].partition_broadcast(128))], W=[gbc.b])
            for t in range(ntiles):
                x, s, xb = xs[t % 2], sm[t % 2], xnb[t % 2]
                kb.dma("sp", [(x.t[:], xd[t * 128:(t + 1) * 128, :])], W=[x.b])
                kb.act(junk.t[:], x.t[:], AF.Square, R=[x.b], W=[junk.b, s.b], accum=s.t[:, 0:1])
                self.rstd_from_ss(s, D)
                kb.stt(xb.t[:], x.t[:], s.t[:, 3:4], gbc.t[:], ALU.mult, ALU.mult, R=[x.b, s.b, gbc.b], W=[xb.b])
                for hh in range(2):
                    p = kb.ps()
                    pv = p.t[:].bitcast(BF16)
                    kb.tr([(pv[:, j * 128:(j + 1) * 128], xb.t[:, (hh * 8 + j) * 128:(hh * 8 + j + 1) * 128]) for j in range(8)],
                          self.ident, R=[xb.b], W=[p.b])
                    kb.copy("act" if hh == 0 else "dve", dst[:, hh * 8:(hh + 1) * 8, t * 128:(t + 1) * 128],
                            pv.rearrange("p (k t) -> p k t", k=8), R=[p.b], W=[])
                    kb.psf(p)
            kb.barrier()

    def gla_alloc(self, es, ntok, full):
        kb = self.kb
        g = {}
        g["lrT"] = [kb.sb(f"g_lrT{i}", [16, ntok], F32, es) for i in range(2 if full else 1)]
        for nm in ("L", "P", "E", "Dd", "X0", "X1"):
            g[nm] = kb.sb("g_" + nm, [128, 512], F32, es)
        g["tot"] = kb.sb("g_tot", [128, 4], F32, es)
        g["edec"] = kb.sb("g_edec", [128, 2, 4], F32, es)
        g["qt"] = [kb.sb(f"g_qt{c}", [128, 512], BF16, es) for c in range(2)]
        g["kt"] = [kb.sb(f"g_kt{c}", [128, 512], BF16, es) for c in range(2)]
        g["khT"] = kb.sb("g_khT", [128, 512], BF16, es)
        g["khat"] = kb.sb("g_khat", [128, 4, 256], BF16, es)
        g["vts"] = [kb.sb(f"g_vt{i}", [128, 4, 512], BF16, es) for i in range(2)]
        g["Sbf"] = [[kb.sb(f"g_Sbf{i}_{c}", [128, 512], BF16, es) for c in range(2)] for i in range(2)]
        g["S2"] = [kb.sb(f"g_S2_{c}", [128, 512], F32, es) for c in range(2)]
        if full:
            g["S"] = [kb.sb(f"g_S{c}", [128, 512], F32, es) for c in range(2)]
            g["attT"] = kb.sb("g_attT", [128, 128], BF16, es)
            g["ot"] = [kb.sb(f"g_ot{i}", [128, 512], F32, es) for i in range(4)]
            g["sm4"] = [kb.sb(f"g_sm4_{i}", [128, 4], F32, es) for i in range(4)]
            g["on"] = kb.sb("g_on", [128, 512], F32, es)
            g["sg"] = kb.sb("g_sg", [128, 512], F32, es)
            g["og"] = kb.sb("g_og", [128, 512], BF16, es)
            g["ogT"] = kb.sb("g_ogT", [128, 4, 512], BF16, es)
            g["gnbc"] = kb.sb("g_gnbc", [128, 512], F32, es)
            g["sm"] = kb.sb("g_sm", [128, 4], F32, es)
            g["junk"] = kb.sb("g_junk", [128, 512], BF16, es)
            kb.dma("sp", [(g["gnbc"].t[:], self.d["gla_norm"].partition_broadcast(128))], W=[g["gnbc"].b])
        return g

    def lr_project(self, xn, ntok, g, dirs):
        kb = self.kb
        for dirn in dirs:
            w = self.W.acquire([("w_lr", 0, D, dirn * 16, 16, 0)])
            for blk in range(ntok // 512):
                p = kb.ps()
                kb.mm([(p.t[0:16, :], w.t[:, kc, 0:16], xn.t[:, kc, blk * 512:(blk + 1) * 512], kc == 0, kc == 15)
                       for kc in range(16)], R=[w.b, xn.b], W=[p.b])
                kb.copy("act", g["lrT"][dirn].t[0:16, blk * 512:(blk + 1) * 512], p.t[0:16, :], R=[p.b], W=[g["lrT"][dirn].b])
                kb.psf(p)
            self.W.release(w)

    def gla_sweep(self, hd, xn, ntok, rev, dirn, mode, S, g):
        kb, W, d = self.kb, self.W, self.d
        outp = mode != "state"
        if outp:
            wqk = W.acquire([("w_in", 0, D, C_QG + hd * 256, 256, 0), ("w_in", 0, D, C_KG + hd * 256, 256, 256)])
        else:
            wqk = W.acquire([("w_in", 0, D, C_KG + hd * 256, 256, 256)])
        wv = W.acquire([("w_in", 0, D, C_VG + hd * 512, 512, 0)]) if mode != "final" else None
        wrg = W.acquire([("w_in", 0, D, C_RG + hd * 512, 512, 0)]) if mode == "final" else None
        lrT = g["lrT"][dirn]
        L, P, E, Dd, X0, X1 = g["L"], g["P"], g["E"], g["Dd"], g["X0"], g["X1"]
        tot, edec = g["tot"], g["edec"]
        Sbfs = g["Sbf"]
        Scur, Salt = [S[0], S[1]], [g["S2"][0], g["S2"][1]]
        sbi = 0
        nblk = ntok // 512
        if outp:
            for c in range(2):
                kb.copy("act", Sbfs[sbi][c].t[:], Scur[c].t[:], R=[Scur[c].b], W=[Sbfs[sbi][c].b])
        for bi in range(nblk):
            blk = (nblk - 1 - bi) if rev else bi
            t0 = blk * 512
            g["vt"] = g["vts"][bi % 2]
            if mode == "final":
                kb._wait("sp", self.vsw)
                kb.dma("sp", [(g["vt"].t[:], d["vsc_d"][hd, :, blk * 4:(blk + 1) * 4, :])], R=[], W=[g["vt"].b])
            pq = []
            if outp:
                for c in range(2):
                    p = kb.ps()
                    kb.mm([(p.t[:, :], wqk.t[:, kc, c * 128:(c + 1) * 128], xn.t[:, kc, t0:t0 + 512], kc == 0, kc == 15)
                           for kc in range(16)], R=[wqk.b, xn.b], W=[p.b])
                    pq.append(p)
            pk = []
            for c in range(2):
                p = kb.ps()
                kb.mm([(p.t[:, :], wqk.t[:, kc, 256 + c * 128:256 + (c + 1) * 128], xn.t[:, kc, t0:t0 + 512], kc == 0, kc == 15)
                       for kc in range(16)], R=[wqk.b, xn.b], W=[p.b])
                pk.append(p)
            pzs = []
            for c in range(2):
                pz = kb.ps()
                kb.mm([(pz.t[:, :], self.wgate.t[0:16, dirn, hd * 256 + c * 128:hd * 256 + (c + 1) * 128],
                        lrT.t[0:16, t0:t0 + 512], True, True)], R=[self.wgate.b, lrT.b], W=[pz.b])
                pzs.append(pz)
            def vmm(tl):
                p = kb.ps()
                kb.mm([(p.t[:, :], xn.t[:, kc, t0 + tl * 128:t0 + (tl + 1) * 128], wv.t[:, kc, :], kc == 0, kc == 15)
                       for kc in range(16)], R=[wv.b, xn.b], W=[p.b])
                return p

            def vcp(tl, p):
                kb.copy("act", g["vt"].t[:, tl, :], p.t[:, :], R=[p.b], W=[g["vt"].b])
                kb.psf(p)

            for c in range(2):
                pv2 = [vmm(2 * c), vmm(2 * c + 1)] if mode != "final" else None
                pz = pzs[c]
                col = dirn * 8 + hd * 2 + c
                kb.act(X0.t[:], pz.t[:], AF.Exp, R=[pz.b, self.nbg.b], W=[X0.b], bias=self.nbg.t[:, col:col + 1], scale=-1.0)
                kb.psf(pz)
                kb.act(L.t[:], X0.t[:], AF.Ln, R=[X0.b, self.cc.b], W=[L.b], bias=self.cc.t[:, 1:2])
                kb.op("dve", lambda: self.nc.vector.tensor_tensor_scan(out=P.t[:], data0=self.rmask.t[:], data1=L.t[:], initial=0.0,
                                                                       op0=ALU.mult, op1=ALU.add), R=[self.rmask.b, L.b], W=[P.b])
                P3 = P.t[:].rearrange("p (t i) -> p t i", i=128)
                kb.copy("dve", tot.t[:].unsqueeze(2), P3[:, :, 127:128], R=[P.b], W=[tot.b])
                totb = tot.t[:].unsqueeze(2).broadcast_to([128, 4, 128])
                E3 = E.t[:].rearrange("p (t i) -> p t i", i=128)
                D3 = Dd.t[:].rearrange("p (t i) -> p t i", i=128)
                if rev:
                    kb.tt(Dd.t[:], P.t[:], L.t[:], ALU.subtract, R=[P.b, L.b], W=[Dd.b])
                    kb.tt(E3, totb, D3, ALU.subtract, R=[tot.b, Dd.b], W=[E.b])
                    Eb = E
                else:
                    kb.tt(D3, totb, P3, ALU.subtract, R=[tot.b, P.b], W=[Dd.b])
                    Eb = P
                kb.act(edec.t[:, c, :], tot.t[:], AF.Exp, R=[tot.b], W=[edec.b], scale=-1.0 / 16.0)
                if outp:
                    kb.act(X0.t[:], Eb.t[:], AF.Exp, R=[Eb.b, self.cc.b], W=[X0.b], bias=self.cc.t[:, 2:3], scale=-1.0 / 16.0)
                    kb.tt(g["qt"][c].t[:], pq[c].t[:], X0.t[:], ALU.mult, R=[pq[c].b, X0.b], W=[g["qt"][c].b])
                    kb.act(X1.t[:], Eb.t[:], AF.Exp, R=[Eb.b], W=[X1.b], scale=1.0 / 16.0)
                    kb.tt(g["kt"][c].t[:], pk[c].t[:], X1.t[:], ALU.mult, R=[pk[c].b, X1.b], W=[g["kt"][c].b])
                kb.act(X0.t[:], Dd.t[:], AF.Exp, R=[Dd.b], W=[X0.b], scale=-1.0 / 16.0)
                kb.tt(g["khT"].t[:], pk[c].t[:], X0.t[:], ALU.mult, R=[pk[c].b, X0.b], W=[g["khT"].b])
                pt = kb.ps()
                ptv = pt.t[:].bitcast(BF16)
                kb.tr([(ptv[:, tl * 128:(tl + 1) * 128], g["khT"].t[:, tl * 128:(tl + 1) * 128]) for tl in range(4)],
                      self.ident, R=[g["khT"].b], W=[pt.b])
                kb.copy("act", g["khat"].t[:, :, c * 128:(c + 1) * 128], ptv[:, 0:512].rearrange("p (t k) -> p t k", t=4),
                        R=[pt.b], W=[g["khat"].b])
                kb.psf(pt)
                if pv2 is not None:
                    vcp(2 * c, pv2[0])
                    vcp(2 * c + 1, pv2[1])
            if mode == "first":
                self.vsw.extend(kb.dma("sp", [(d["vsc_d"][hd, :, blk * 4:(blk + 1) * 4, :], g["vt"].t[:])], R=[g["vt"].b], W=[]))
            if outp:
                kb.psf(*pq)
            kb.psf(*pk)
            if mode == "final":
                kb._wait("sp", self.obw)
                for ti in range(4):
                    tl = (3 - ti) if rev else ti
                    tok = t0 + tl * 128
                    kb.dma("sp", [(g["ot"][tl].t[:], d["ob_d"][hd, tok:tok + 128, :])], R=[], W=[g["ot"][tl].b])
            for ti in range(4):
                tl = (3 - ti) if rev else ti
                tok = t0 + tl * 128
                sl = slice(tl * 128, (tl + 1) * 128)
                Sbf = Sbfs[sbi]
                if outp:
                    ps_s = kb.ps()
                    kb.mm([(ps_s.t[:, 0:128], g["kt"][c].t[:, sl], g["qt"][c].t[:, sl], c == 0, c == 1) for c in range(2)],
                          R=[g["kt"][0].b, g["kt"][1].b, g["qt"][0].b, g["qt"][1].b], W=[ps_s.b])
                    kb.tt(g["attT"].t[:], ps_s.t[:, 0:128], self.trim.t[:, 1 if rev else 0, :], ALU.mult,
                          R=[ps_s.b, self.trim.b], W=[g["attT"].b])
                    kb.psf(ps_s)
                psts = []
                for c in range(2):
                    pst = kb.ps()
                    kb.mm([(pst.t[:, :], g["khat"].t[:, tl, c * 128:(c + 1) * 128], g["vt"].t[:, tl, :], True, True)],
                          R=[g["khat"].b, g["vt"].b], W=[pst.b])
                    psts.append(pst)
                if outp:
                    ps_o = kb.ps()
                    kb.mm([(ps_o.t[:, :], g["attT"].t[:], g["vt"].t[:, tl, :], True, False),
                           (ps_o.t[:, :], g["qt"][0].t[:, sl], Sbf[0].t[:], False, False),
                           (ps_o.t[:, :], g["qt"][1].t[:, sl], Sbf[1].t[:], False, True)],
                          R=[g["attT"].b, g["vt"].b, g["qt"][0].b, g["qt"][1].b, Sbf[0].b, Sbf[1].b], W=[ps_o.b])
                for c in range(2):
                    kb.stt(Salt[c].t[:], Scur[c].t[:], edec.t[:, c, tl:tl + 1], psts[c].t[:, :], ALU.mult, ALU.add,
                           R=[Scur[c].b, edec.b, psts[c].b], W=[Salt[c].b])
                    kb.psf(psts[c])
                    if outp:
                        kb.copy("act", Sbfs[1 - sbi][c].t[:], Salt[c].t[:], R=[Salt[c].b], W=[Sbfs[1 - sbi][c].b])
                Scur, Salt = Salt, Scur
                sbi = 1 - sbi
                if mode == "first":
                    ot = g["ot"][ti % 2]
                    kb.copy("act", ot.t[:], ps_o.t[:, :], R=[ps_o.b], W=[ot.b])
                    kb.psf(ps_o)
                    self.obw.extend(kb.dma("sp", [(d["ob_d"][hd, tok:tok + 128, :], ot.t[:])], R=[ot.b], W=[]))
                elif mode == "final":
                    ot = g["ot"][tl]
                    kb.tt(ot.t[:], ps_o.t[:, :], ot.t[:], ALU.add, R=[ps_o.b, ot.b], W=[ot.b])
                    kb.psf(ps_o)
            if mode == "final":
                def rgmm(tl_):
                    tok_ = t0 + tl_ * 128
                    pr_ = kb.ps()
                    kb.mm([(pr_.t[:, :], xn.t[:, kc, tok_:tok_ + 128], wrg.t[:, kc, :], kc == 0, kc == 15) for kc in range(16)],
                          R=[wrg.b, xn.b], W=[pr_.b])
                    return pr_

                pr_next = rgmm(0)
                for tl in range(4):
                    sm = g["sm4"][tl]
                    kb.act(g["junk"].t[:], g["ot"][tl].t[:], AF.Square, R=[g["ot"][tl].b], W=[g["junk"].b, sm.b], accum=sm.t[:, 0:1])
                for tl in range(4):
                    sm = g["sm4"][tl]
                    kb.ts(sm.t[:, 1:2], sm.t[:, 0:1], 1.0 / 512, EPS, ALU.mult, ALU.add, R=[sm.b], W=[sm.b])
                    kb.act(sm.t[:, 2:3], sm.t[:, 1:2], AF.Ln, R=[sm.b], W=[sm.b])
                    kb.act(sm.t[:, 3:4], sm.t[:, 2:3], AF.Exp, R=[sm.b], W=[sm.b], scale=-0.5)
                for tl in range(4):
                    tok = t0 + tl * 128
                    sl = slice(tl * 128, (tl + 1) * 128)
                    sm = g["sm4"][tl]
                    ot = g["ot"][tl]
                    kb.stt(g["on"].t[:], ot.t[:], sm.t[:, 3:4], g["gnbc"].t[:], ALU.mult, ALU.mult,
                           R=[ot.b, sm.b, g["gnbc"].b], W=[g["on"].b])
                    pr = pr_next
                    pr_next = rgmm(tl + 1) if tl < 3 else None
                    kb.act(g["sg"].t[:], pr.t[:, :], AF.Silu, R=[pr.b], W=[g["sg"].b])
                    kb.psf(pr)
                    kb.tt(g["og"].t[:], g["on"].t[:], g["sg"].t[:], ALU.mult, R=[g["on"].b, g["sg"].b], W=[g["og"].b])
                    pt = kb.ps()
                    ptv = pt.t[:].bitcast(BF16)
                    kb.tr([(ptv[:, c4 * 128:(c4 + 1) * 128], g["og"].t[:, c4 * 128:(c4 + 1) * 128]) for c4 in range(4)],
                          self.ident, R=[g["og"].b], W=[pt.b])
                    kb.copy("dve", g["ogT"].t[:, :, sl], ptv[:, 0:512].rearrange("p (c t) -> p c t", c=4), R=[pt.b], W=[g["ogT"].b])
                    kb.psf(pt)
                kb.dma("sp", [(d["ogT_d"][hd * 4:(hd + 1) * 4, :, t0:t0 + 512].rearrange("c p t -> p c t"), g["ogT"].t[:])],
                       R=[g["ogT"].b], W=[])
        assert Scur[0] is S[0]
        W.release(wqk)
        if wv is not None:
            W.release(wv)
        if wrg is not None:
            W.release(wrg)

    def attn_alloc(self, es, halo_only):
        kb = self.kb
        a = {}
        n = 1024 if halo_only else 4096
        a["kT"] = kb.sb("a_kT", [128, n], BF16, es)
        a["vT"] = kb.sb("a_vT", [128, n], BF16, es)
        a["sq"] = kb.sb("a_sq", [128, 512], BF16, es)
        a["xg"] = kb.sb("a_xg", [128, 512], F32, es)
        a["rt"] = kb.sb("a_rt", [128, 512], F32, es)
        a["xgb"] = kb.sb("a_xgb", [32, 512], BF16, es)
        a["t1"] = kb.sb("a_t1", [32, 512], F32, es)
        a["t2"] = kb.sb("a_t2", [32, 512], F32, es)
        a["cs"] = [kb.sb(f"a_cs{i}", [32, 512], F32, es) for i in range(2)]
        a["sn"] = [kb.sb(f"a_sn{i}", [32, 512], F32, es) for i in range(2)]
        a["tabi"] = 0
        if not halo_only:
            a["qT"] = kb.sb("a_qT", [128, T], BF16, es)
            a["V"] = kb.sb("a_V", [128, 32, 128], BF16, es)
            a["acc"] = kb.sb("a_acc", [128, 2, T], F32, es)
            a["pT"] = [kb.sb(f"a_pT{i}", [128, 256], BF16, es) for i in range(2)]
            a["pm"] = [kb.sb(f"a_pm{i}", [128, 256], BF16, es) for i in range(2)]
            a["oo"] = kb.sb("a_oo", [128, T], BF16, es)
            a["qi"] = 0
        return a

    def norm_rope(self, praw, n, gcol, tab_off, out_ap, out_b, a):
        kb = self.kb
        sq, xg, rt, t1, t2 = a["sq"], a["xg"], a["rt"], a["t1"], a["t2"]
        i = a["tabi"]
        a["tabi"] = 1 - i
        cs, sn = a["cs"][i], a["sn"][i]
        kb.dma("sp", [(cs.t[:, 0:n], self.d["cos_t"][:, tab_off:tab_off + n])], W=[cs.b])
        kb.dma("sp", [(sn.t[:, 0:n], self.d["sin_t"][:, tab_off:tab_off + n])], W=[sn.b])
        kb.act(sq.t[:, 0:n], praw.t[:, 0:n], AF.Square, R=[praw.b], W=[sq.b])
        kb.act(xg.t[:, 0:n], praw.t[:, 0:n], AF.Copy, R=[praw.b, self.qkg.b], W=[xg.b], scale=self.qkg.t[:, gcol:gcol + 1])
        xgb = a["xgb"]
        kb.act(xgb.t[:, 0:n], praw.t[0:32, 0:n], AF.Copy, R=[praw.b, self.qkg.b], W=[xgb.b], scale=self.qkg.t[0:32, gcol:gcol + 1])
        psum_ = kb.ps()
        kb.mm([(psum_.t[:, 0:n], self.onesb.t[:], sq.t[:, 0:n], True, True)], R=[self.onesb.b, sq.b], W=[psum_.b])
        psw = kb.ps()
        kb.mm([(psw.t[0:32, 0:n], self.permb.t[0:32, 0:32], xgb.t[0:32, 0:n], True, True)], R=[self.permb.b, xgb.b], W=[psw.b])
        kb.act(rt.t[:, 0:n], psum_.t[:, 0:n], AF.Ln, R=[psum_.b, self.cc.b], W=[rt.b], bias=self.cc.t[:, 0:1], scale=1.0 / 128.0)
        kb.psf(psum_)
        kb.act(rt.t[:, 0:n], rt.t[:, 0:n], AF.Exp, R=[rt.b], W=[rt.b], scale=-0.5)
        kb.tt(t1.t[:, 0:n], xg.t[0:32, 0:n], cs.t[:, 0:n], ALU.mult, R=[xg.b, cs.b], W=[t1.b])
        kb.tt(t2.t[:, 0:n], psw.t[0:32, 0:n], sn.t[:, 0:n], ALU.mult, R=[psw.b, sn.b], W=[t2.b])
        kb.psf(psw)
        kb.tt(xg.t[0:32, 0:n], t1.t[:, 0:n], t2.t[:, 0:n], ALU.add, R=[t1.b, t2.b], W=[xg.b])
        kb.tt(out_ap, xg.t[:, 0:n], rt.t[:, 0:n], ALU.mult, R=[xg.b, rt.b], W=[out_b])

    def attn_w(self, h, gi, with_q):
        col = gi * 1024 + h * 128
        spec = []
        if with_q:
            spec.append(("w_in", 0, D, C_QA + col, 128, 0))
        spec.append(("w_in", 0, D, C_KA + col, 128, 128))
        spec.append(("w_in", 0, D, C_VA + col, 128, 256))
        return self.W.acquire(spec)

    def attn_halo(self, h, gi, xc, a):
        kb = self.kb
        r = DIL[gi]
        halo = 64 * r
        w = self.attn_w(h, gi, False)
        idx = gi * 8 + h
        done = 0
        while done < halo:
            n = min(512, halo - done)
            x0 = 1024 - halo + done
            p = kb.ps()
            kb.mm([(p.t[:, 0:n], w.t[:, kc, 128:256], xc.t[:, kc, x0:x0 + n], kc == 0, kc == 15) for kc in range(16)],
                  R=[w.b, xc.b], W=[p.b])
            self.norm_rope(p, n, 3 + gi, x0, a["kT"].t[:, done:done + n], a["kT"].b, a)
            kb.psf(p)
            p = kb.ps()
            kb.mm([(p.t[:, 0:n], w.t[:, kc, 256:384], xc.t[:, kc, x0:x0 + n], kc == 0, kc == 15) for kc in range(16)],
                  R=[w.b, xc.b], W=[p.b])
            kb.copy("act", a["vT"].t[:, done:done + n], p.t[:, 0:n], R=[p.b], W=[a["vT"].b])
            kb.psf(p)
            done += n
        self.W.release(w)
        kb.dma("sp", [(self.d["kh_d"][idx, :, 0:halo], a["kT"].t[:, 0:halo])], R=[a["kT"].b], W=[])
        kb.dma("sp", [(self.d["vh_d"][idx, :, 0:halo], a["vT"].t[:, 0:halo])], R=[a["vT"].b], W=[])

    def attn_head(self, h, gi, a):
        kb = self.kb
        r = DIL[gi]
        halo = 64 * r
        Wn = T + 128 * r
        idx = gi * 8 + h
        kT, vT, qT, V, acc = a["kT"], a["vT"], a["qT"], a["V"], a["acc"]
        kb.memset("dve", kT.t[:, halo + T:Wn], 0.0, [kT.b])
        kb.memset("dve", vT.t[:, halo + T:Wn], 0.0, [vT.b])
        kb.dma("sp", [(kT.t[:, 0:halo], self.d["kh_d"][idx, :, 0:halo])], W=[kT.b])
        kb.dma("sp", [(vT.t[:, 0:halo], self.d["vh_d"][idx, :, 0:halo])], W=[vT.b])
        w = self.attn_w(h, gi, True)
        items = [(blk, which) for blk in range(4) for which in range(3)]

        def proj(it):
            blk, which = it
            t0 = blk * 512
            p = kb.ps()
            kb.mm([(p.t[:, :], w.t[:, kc, which * 128:(which + 1) * 128], self.xn.t[:, kc, t0:t0 + 512], kc == 0, kc == 15)
                   for kc in range(16)], R=[w.b, self.xn.b], W=[p.b])
            return p

        def post(it, p):
            blk, which = it
            t0 = blk * 512
            if which == 0:
                self.norm_rope(p, 512, gi, 1024 + t0, qT.t[:, t0:t0 + 512], qT.b, a)
            elif which == 1:
                self.norm_rope(p, 512, 3 + gi, 1024 + t0, kT.t[:, halo + t0:halo + t0 + 512], kT.b, a)
            else:
                kb.copy("act", vT.t[:, halo + t0:halo + t0 + 512], p.t[:, :], R=[p.b], W=[vT.b])
            kb.psf(p)

        pcur = proj(items[0])
        for ii, it in enumerate(items):
            pnext = proj(items[ii + 1]) if ii + 1 < len(items) else None
            post(it, pcur)
            pcur = pnext
        self.W.release(w)
        nq = T // (128 * r)
        vi = 0
        for c in range(r):
            for j in range(nq):
                p = kb.ps()
                pv = p.t[:].bitcast(BF16)
                base = c + r * 128 * j
                inA = [vT.t[:, base + r * 64 * bb: base + r * 64 * bb + r * 63 + 1: r] for bb in (0, 3)]
                inB = vT.t[:, base + r * 64: base + r * 64 + r * 127 + 1: r]
                kb.tr([(pv[0:64, 0:128], inA[0]), (pv[64:128, 0:128], inA[1]), (pv[:, 128:256], inB)],
                      self.ident, R=[vT.b], W=[p.b])
                kb.copy("act" if (vi % 2 == 0) else "dve", V.t[:, 2 * vi:2 * vi + 2, :], pv[:, 0:256].rearrange("p (a d) -> p a d", a=2),
                        R=[p.b], W=[V.b])
                kb.psf(p)
                vi += 1
        sc = float(128.0 ** -0.5)
        tiles = [(c, j) for c in range(r) for j in range(nq)]

        def scores(vi):
            c, j = tiles[vi]
            first, last = (j == 0), (j == nq - 1)
            mi = 3 if (first and last) else (0 if first else (2 if last else 1))
            base = c + r * 128 * j
            qs = qT.t[:, base: base + r * 127 + 1: r]
            ps_s = kb.ps()
            kA = [kT.t[:, base + r * 64 * bb: base + r * 64 * bb + r * 63 + 1: r] for bb in (0, 3)]
            kB = kT.t[:, base + r * 64: base + r * 64 + r * 127 + 1: r]
            kb.mm([(ps_s.t[0:64, 0:128], kA[0], qs, True, True), (ps_s.t[64:128, 0:128], kA[1], qs, True, True),
                   (ps_s.t[:, 128:256], kB, qs, True, True)], R=[kT.b, qT.b], W=[ps_s.b])
            i = a["qi"]
            a["qi"] = 1 - i
            pT, pm = a["pT"][i], a["pm"][i]
            kb.act(pT.t[:], ps_s.t[:, 0:256], AF.Exp, R=[ps_s.b], W=[pT.b], scale=sc)
            kb.psf(ps_s)
            kb.tt(pm.t[:], pT.t[:], self.amask.t[:, mi, :], ALU.mult, R=[pT.b, self.amask.b], W=[pm.b])
            return pm

        def pv(vi, pm):
            c, j = tiles[vi]
            base = c + r * 128 * j
            ps_o = kb.ps()
            kb.mm([(ps_o.t[:, 0:128], V.t[:, 2 * vi, :], pm.t[:, 0:128], True, False),
                   (ps_o.t[:, 0:128], V.t[:, 2 * vi + 1, :], pm.t[:, 128:256], False, True),
                   (ps_o.t[:, 128:256], self.onesb.t[:], pm.t[:, 0:128], True, False),
                   (ps_o.t[:, 128:256], self.onesb.t[:], pm.t[:, 128:256], False, True)],
                  R=[V.b, pm.b, self.onesb.b], W=[ps_o.b])
            dst = acc.t[:, :, base: base + r * 127 + 1: r]
            src = ps_o.t[:, 0:256].rearrange("p (a q) -> p a q", a=2)
            if gi == 0:
                kb.copy("act", dst, src, R=[ps_o.b], W=[acc.b])
            else:
                kb.tt(dst, dst, src, ALU.add, R=[ps_o.b, acc.b], W=[acc.b])
            kb.psf(ps_o)

        prev = None
        for vi in range(len(tiles)):
            pm = scores(vi)
            if prev is not None:
                pv(*prev)
            prev = (vi, pm)
        pv(*prev)

    def attn_finish(self, h, a):
        kb = self.kb
        acc, oo = a["acc"], a["oo"]
        kb.act(acc.t[:, 1, :], acc.t[:, 1, :], AF.Ln, R=[acc.b], W=[acc.b])
        kb.act(acc.t[:, 1, :], acc.t[:, 1, :], AF.Exp, R=[acc.b], W=[acc.b], scale=-1.0)
        kb.tt(oo.t[:], acc.t[:, 0, :], acc.t[:, 1, :], ALU.mult, R=[acc.b], W=[oo.b])
        kb.dma("sp", [(self.d["oaT_d"][h], oo.t[:])], R=[oo.b], W=[])

    def merge_stage(self):
        kb, d, W = self.kb, self.d, self.W
        with ExitStack() as es:
            oas = [kb.sb(f"m_oa{i}", [128, 8, 512], BF16, es) for i in range(2)]
            ogs = [kb.sb(f"m_og{i}", [128, 16, 512], BF16, es) for i in range(2)]

            def mload(blk_):
                t0_ = blk_ * 512
                kb.dma("sp", [(oas[blk_ % 2].t[:], d["oaT_d"][:, :, t0_:t0_ + 512].rearrange("c p t -> p c t"))], W=[oas[blk_ % 2].b])
                kb.dma("sp", [(ogs[blk_ % 2].t[:], d["ogT_d"][:, :, t0_:t0_ + 512].rearrange("c p t -> p c t"))], W=[ogs[blk_ % 2].b])

            mload(0)
            mT = kb.sb("m_mT", [128, 16, 512], BF16, es)
            sa = kb.sb("m_sa", [128, 512], F32, es)
            sb_ = kb.sb("m_sb", [128, 512], F32, es)
            ta = kb.sb("m_ta", [128, 512], F32, es)
            tb_ = kb.sb("m_tb", [128, 512], F32, es)
            for blk in range(4):
                t0 = blk * 512
                oa, og = oas[blk % 2], ogs[blk % 2]
                for fc in range(16):
                    c0 = fc * 128
                    if fc == 4 and blk + 1 < 4:
                        mload(blk + 1)
                    w = W.acquire([("@c_m", fc, 56 * 128)])
                    wv = w.t[:].rearrange("p k (a c) -> p (k a) c", c=128)
                    pua, pub, pga, pgb = kb.ps(), kb.ps(), kb.ps(), kb.ps()
                    kb.mm([(pua.t[:, :], wv[:, kc, :], oa.t[:, kc, :], kc == 0, kc == 7) for kc in range(8)], R=[w.b, oa.b], W=[pua.b])
                    kb.mm([(pub.t[:, :], wv[:, 8 + kc, :], og.t[:, kc, :], kc == 0, kc == 15) for kc in range(16)], R=[w.b, og.b], W=[pub.b])
                    kb.mm([(pga.t[:, :], wv[:, 24 + kc, :], self.xn.t[:, kc, t0:t0 + 512], kc == 0, kc == 15) for kc in range(16)],
                          R=[w.b, self.xn.b], W=[pga.b])
                    kb.mm([(pgb.t[:, :], wv[:, 40 + kc, :], self.xn.t[:, kc, t0:t0 + 512], kc == 0, kc == 15) for kc in range(16)],
                          R=[w.b, self.xn.b], W=[pgb.b])
                    W.release(w)
                    kb.act(sa.t[:], pga.t[:, :], AF.Sigmoid, R=[pga.b], W=[sa.b])
                    kb.act(sb_.t[:], pgb.t[:, :], AF.Sigmoid, R=[pgb.b], W=[sb_.b])
                    kb.tt(ta.t[:], sa.t[:], pua.t[:, :], ALU.mult, R=[sa.b, pua.b], W=[ta.b])
                    kb.tt(tb_.t[:], sb_.t[:], pub.t[:, :], ALU.mult, R=[sb_.b, pub.b], W=[tb_.b])
                    kb.psf(pua, pub, pga, pgb)
                    kb.tt(mT.t[:, fc, :], ta.t[:], tb_.t[:], ALU.add, R=[ta.b, tb_.b], W=[mT.b])
                kb.dma("sp", [(d["mT_d"][:, :, t0:t0 + 512].rearrange("c p t -> p c t"), mT.t[:])], R=[mT.b], W=[])
            kb.barrier()

    def out_stage(self):
        kb, d, W = self.kb, self.d, self.W
        with ExitStack() as es:
            mTs = [kb.sb(f"o_mT{i}", [128, 16, 512], BF16, es) for i in range(2)]
            hss = [[kb.sb(f"o_h{j}_{i}", [128, D], F32, es) for i in range(4)] for j in range(2)]
            junk = kb.sb("o_junk", [128, D], BF16, es)
            hn = kb.sb("o_hn", [128, D], BF16, es)
            hnT = kb.sb("o_hnT", [128, 16, 128], BF16, es)
            gbc = kb.sb("o_gbc", [128, D], F32, es)
            sm = kb.sb("o_sm", [128, 4], F32, es)
            kb.dma("sp", [(gbc.t[:], d["norm_ffn"].partition_broadcast(128))], W=[gbc.b])

            def mmblk(blk):
                t0 = blk * 512
                mT, hs = mTs[blk % 2], hss[blk % 2]
                kb.dma("sp", [(mT.t[:], d["mT_d"][:, :, t0:t0 + 512].rearrange("c p t -> p c t"))], W=[mT.b])
                for tl in range(4):
                    kb.dma("sp", [(hs[tl].t[:], d["x_own"][t0 + tl * 128:t0 + (tl + 1) * 128, :])], W=[hs[tl].b])
                for cb in range(4):
                    w = W.acquire([("@c_o", cb, 8192)])
                    for tl in range(4):
                        p = kb.ps()
                        kb.mm([(p.t[:, :], mT.t[:, kc, tl * 128:(tl + 1) * 128], w.t[:, kc, :], kc == 0, kc == 15) for kc in range(16)],
                              R=[w.b, mT.b], W=[p.b])
                        hv = hs[tl].t[:, cb * 512:(cb + 1) * 512]
                        kb.tt(hv, hv, p.t[:, :], ALU.add, R=[p.b, hs[tl].b], W=[hs[tl].b])
                        kb.psf(p)
                    W.release(w)

            def postblk(blk):
                t0 = blk * 512
                hs = hss[blk % 2]
                for tl in range(4):
                    h = hs[tl]
                    tok = t0 + tl * 128
                    kb.dma("sp", [(d["h_d"][tok:tok + 128, :], h.t[:])], R=[h.b], W=[])
                    kb.act(junk.t[:], h.t[:], AF.Square, R=[h.b], W=[junk.b, sm.b], accum=sm.t[:, 0:1])
                    self.rstd_from_ss(sm, D)
                    kb.stt(hn.t[:], h.t[:], sm.t[:, 3:4], gbc.t[:], ALU.mult, ALU.mult, R=[h.b, sm.b, gbc.b], W=[hn.b])
                    for hh in range(2):
                        p = kb.ps()
                        pv = p.t[:].bitcast(BF16)
                        kb.tr([(pv[:, j * 128:(j + 1) * 128], hn.t[:, (hh * 8 + j) * 128:(hh * 8 + j + 1) * 128]) for j in range(8)],
                              self.ident, R=[hn.b], W=[p.b])
                        kb.copy("act" if hh == 0 else "dve", hnT.t[:, hh * 8:(hh + 1) * 8, :], pv.rearrange("p (k t) -> p k t", k=8),
                                R=[p.b], W=[hnT.b])
                        kb.psf(p)
                    kb.dma("sp", [(d["hnT_d"][:, :, tok:tok + 128].rearrange("c p t -> p c t"), hnT.t[:])], R=[hnT.b], W=[])

            mmblk(0)
            for blk in range(4):
                if blk + 1 < 4:
                    mmblk(blk + 1)
                postblk(blk)
            kb.barrier()

    def ffn_stage(self):
        kb, d, W = self.kb, self.d, self.W
        with ExitStack() as es:
            hnTs = [kb.sb(f"f_hnT{i}", [128, 16, 512], BF16, es) for i in range(2)]
            kb.dma("sp", [(hnTs[0].t[:], d["hnT_d"][:, :, 0:512].rearrange("c p t -> p c t"))], W=[hnTs[0].b])
            aT = kb.sb("f_aT", [128, 64, 512], BF16, es)
            hs = [kb.sb(f"f_h{i}", [128, D], F32, es) for i in range(4)]
            rr = [kb.sb(f"f_r{i}", [128, 512], F32, es) for i in range(2)]
            for blk in range(4):
                t0 = blk * 512
                hnT = hnTs[blk % 2]
                n = 0
                for fg in range(16):
                    w = W.acquire([("@c_f1", fg, 8192)])
                    for c4 in range(4):
                        p = kb.ps()
                        kb.mm([(p.t[:, :], w.t[:, kc, c4 * 128:(c4 + 1) * 128], hnT.t[:, kc, :], kc == 0, kc == 15) for kc in range(16)],
                              R=[w.b, hnT.b], W=[p.b])
                        r_ = rr[n % 2]
                        n += 1
                        kb.act(r_.t[:], p.t[:, :], AF.Relu, R=[p.b], W=[r_.b])
                        kb.psf(p)
                        kb.tt(aT.t[:, fg * 4 + c4, :], r_.t[:], r_.t[:], ALU.mult, R=[r_.b], W=[aT.b])
                    W.release(w)
                for tl in range(4):
                    kb.dma("sp", [(hs[tl].t[:], d["h_d"][t0 + tl * 128:t0 + (tl + 1) * 128, :])], W=[hs[tl].b])
                if blk + 1 < 4:
                    nb = hnTs[(blk + 1) % 2]
                    kb.dma("sp", [(nb.t[:], d["hnT_d"][:, :, t0 + 512:t0 + 1024].rearrange("c p t -> p c t"))], W=[nb.b])
                for cb in range(4):
                    pacc = [kb.ps() for _ in range(4)]
                    for kg in range(4):
                        w = W.acquire([("@c_f2", cb * 4 + kg, 8192)])
                        for tl in range(4):
                            kb.mm([(pacc[tl].t[:, :], aT.t[:, kg * 16 + kc, tl * 128:(tl + 1) * 128], w.t[:, kc, :],
                                    kg == 0 and kc == 0, kg == 3 and kc == 15) for kc in range(16)],
                                  R=[w.b, aT.b], W=[pacc[tl].b])
                        W.release(w)
                    for tl in range(4):
                        hv = hs[tl].t[:, cb * 512:(cb + 1) * 512]
                        kb.tt(hv, hv, pacc[tl].t[:, :], ALU.add, R=[pacc[tl].b, hs[tl].b], W=[hs[tl].b])
                    kb.psf(*pacc)
                for tl in range(4):
                    tok = t0 + tl * 128
                    kb.dma("sp", [(d["y"][tok:tok + 128, :], hs[tl].t[:])], R=[hs[tl].b], W=[])
            kb.barrier()


_CACHE = {}


def get_program(dbg=False):
    import os
    STOP[0] = int(os.environ.get("KSTOP", "0"))
    if dbg not in _CACHE:
        b0 = Builder(dbg, None)
        b0.build()
        sched = list(b0.W.rec)
        b1 = Builder(dbg, sched)
        nc = b1.build()
        _CACHE[dbg] = nc
    return _CACHE[dbg]


def core_inputs(c, x_prompt, x_sample, shared, w_in, w_gla_gate, b_gla_gate):
    if c < 4:
        x_own = x_prompt[c]
        x_ctx = np.zeros((T, D), np.float32)
        rev, has_halo = False, False
        pos_own = np.arange(T)
        pos_tail = np.zeros(1024, np.int64)
    else:
        s, half = (c - 4) // 2, (c - 4) % 2
        if half == 1:
            x_own = x_sample[s, T:2 * T]
            x_ctx = x_sample[s, 0:T]
            rev = False
            pos_own = np.arange(T, 2 * T)
            pos_tail = np.arange(T - 1024, T)
        else:
            x_own = x_sample[s, 0:T][::-1]
            x_ctx = x_sample[s, T:2 * T][::-1]
            rev = True
            pos_own = np.arange(T)[::-1]
            pos_tail = np.arange(T, T + 1024)[::-1]
        has_halo = True
    di = 1 if rev else 0
    lrc = (C_LRF, C_LRB)
    w_lr = np.concatenate([w_in[:, lrc[di]:lrc[di] + 16], w_in[:, lrc[1 - di]:lrc[1 - di] + 16]], axis=1)
    w_gate = np.stack([w_gla_gate[di], w_gla_gate[1 - di]], axis=1)
    bg = np.stack([b_gla_gate[di], b_gla_gate[1 - di]], axis=0)
    b_gate = bg.reshape(2, 8, 128).transpose(2, 0, 1).reshape(128, 16)
    pos = np.concatenate([pos_tail, pos_own]).astype(np.float32)
    inv_freq = (np.float32(500000.0) ** (-np.arange(0, 32, 2, dtype=np.float32) / np.float32(32))).astype(np.float32)
    ang = (pos[:, None] * inv_freq[None, :]).astype(np.float32)
    cs = np.cos(ang.astype(np.float64)).astype(np.float32).T
    sn = np.sin(ang.astype(np.float64)).astype(np.float32).T
    cos_t = np.concatenate([cs, cs], axis=0)
    sin_t = np.concatenate([-sn, sn], axis=0)
    am = np.zeros((128, 4, 256), np.float32)
    aidx = np.arange(128)[None, :]
    b = np.arange(64)[:, None]
    A_lo = (b >= aidx).astype(np.float32)
    A_hi = (aidx >= b + 64).astype(np.float32)
    bb = np.arange(128)[:, None]
    Bm = (np.abs(bb - aidx) <= 64).astype(np.float32)
    for v in range(4):
        first = v in (0, 3)
        last = v in (2, 3)
        am[0:64, v, 0:128] = A_lo * (1.0 if (not first or has_halo) else 0.0)
        am[64:128, v, 0:128] = A_hi * (0.0 if last else 1.0)
        am[:, v, 128:256] = Bm
    d = dict(shared)
    d.update(x_own=np.ascontiguousarray(x_own), x_ctx=np.ascontiguousarray(x_ctx), w_lr=np.ascontiguousarray(w_lr),
             w_gate=np.ascontiguousarray(w_gate), b_gate=np.ascontiguousarray(b_gate),
             cos_t=np.ascontiguousarray(cos_t), sin_t=np.ascontiguousarray(sin_t), amask=am)
    return d, rev


def shared_inputs(norm_mix, w_in, q_norm, k_norm, gla_norm, w_branch_attn, w_branch_gla, w_out, norm_ffn, w_ff1, w_ff2):
    perm = np.zeros((32, 32), np.float32)
    for i in range(16):
        perm[16 + i, i] = 1.0
        perm[i, 16 + i] = 1.0
    jj = np.arange(128)[:, None]
    ii = np.arange(128)[None, :]
    trim = np.stack([(jj <= ii).astype(np.float32), (jj > ii).astype(np.float32)], axis=1)
    rmask = np.ones((128, 512), np.float32)
    rmask[:, 0::128] = 0.0
    qk_gain = np.concatenate([q_norm.T, k_norm.T], axis=1)
    return dict(w_in=w_in, norm_mix=norm_mix, norm_ffn=norm_ffn, gla_norm=gla_norm, qk_gain=np.ascontiguousarray(qk_gain),
                w_ba=w_branch_attn, w_bg=w_branch_gla, w_out=w_out, w_ff1=w_ff1, w_ff2=w_ff2,
                ident=np.eye(128, dtype=np.float32), perm=perm, trim=np.ascontiguousarray(trim), rmask=rmask)


def kernel(x_prompt, x_sample, norm_mix, w_in, q_norm, k_norm, w_gla_gate, b_gla_gate, gla_norm,
           w_branch_attn, w_branch_gla, w_out, norm_ffn, w_ff1, w_ff2, _cores=None, _dbg=False):
    f = lambda a: np.ascontiguousarray(np.asarray(a, dtype=np.float32))
    x_prompt, x_sample = f(x_prompt), f(x_sample)
    shared = shared_inputs(f(norm_mix)[0], f(w_in)[0], f(q_norm)[0], f(k_norm)[0], f(gla_norm)[0], f(w_branch_attn)[0],
                           f(w_branch_gla)[0], f(w_out)[0], f(norm_ffn)[0], f(w_ff1)[0], f(w_ff2)[0])
    cores = list(range(NCORES)) if _cores is None else list(_cores)
    in_maps, revs = [], []
    for c in cores:
        m, rev = core_inputs(c, x_prompt, x_sample, shared, shared["w_in"], f(w_gla_gate)[0], f(b_gla_gate)[0])
        in_maps.append(m)
        revs.append(rev)
    nc = get_program(_dbg)
    res = run_bass_kernel_spmd(nc, in_maps, core_ids=list(range(len(cores))))
    if _dbg:
        return res, revs
    y_prompt = np.zeros((4, T, D), np.float32)
    y_sample = np.zeros((2, 2 * T, D), np.float32)
    for c, r, rev in zip(cores, res.results, revs):
        y = np.asarray(r["y"], dtype=np.float32)
        if rev:
            y = y[::-1]
        if c < 4:
            y_prompt[c] = y
        else:
            s, half = (c - 4) // 2, (c - 4) % 2
            y_sample[s, half * T:(half + 1) * T] = y
    return (y_prompt, y_sample)
```

```python
import numpy as np
from contextlib import ExitStack
import concourse.bass as bass
import concourse.mybir as mybir
from concourse.bass_utils import run_bass_kernel_spmd

F32 = mybir.dt.float32
BF16 = mybir.dt.bfloat16
AF = mybir.ActivationFunctionType
ALU = mybir.AluOpType

D = 2048
T = 2048
NCORES = 8
INC = 19488
C_QA, C_KA, C_VA = 0, 3072, 6144
C_QG, C_KG, C_VG, C_RG = 9216, 10240, 11264, 13312
C_LRF, C_LRB, C_GA, C_GB = 15360, 15376, 15392, 17440
DIL = (1, 4, 16)
EPS = 1e-6
NSLOT = 3


class Buf:
    __slots__ = ("name", "w", "r")

    def __init__(self, name):
        self.name = name
        self.w = []
        self.r = {}


class TB:
    def __init__(self, t, name):
        self.t = t
        self.b = Buf(name)


class DSem:
    def __init__(self, sem, key):
        self.sem = sem
        self.key = key
        self.count = 0


class KB:
    def __init__(self, nc, es):
        self.nc = nc
        self.engs = {"pe": nc.tensor, "act": nc.scalar, "dve": nc.vector, "pool": nc.gpsimd, "sp": nc.sync}
        self.sem = {k: es.enter_context(nc.semaphore("sem_" + k)) for k in self.engs}
        self.cnt = {k: 0 for k in self.engs}
        self.seen = {k: {} for k in self.engs}
        self.dsems = {"sp": [DSem(es.enter_context(nc.semaphore(f"dsp{i}")), f"dsp{i}") for i in range(8)],
                      "pool": [DSem(es.enter_context(nc.semaphore(f"dpl{i}")), f"dpl{i}") for i in range(8)]}
        self.didx = {"sp": 0, "pool": 0, "pc": 0}
        self.dsems["pc"] = [DSem(es.enter_context(nc.semaphore(f"dpc{i}")), f"dpc{i}") for i in range(4)]
        self.engs["pc"] = nc.gpsimd
        self.seen["pc"] = self.seen["pool"]
        self.psb = [TB(es.enter_context(nc.psum_tensor(f"psb{i}", [128, 512], F32)), f"psb{i}") for i in range(8)]
        self.psfree = list(range(8))
        self.nins = 0
        self.mute = False

    def sb(self, name, shape, dt, es):
        self.nsb = getattr(self, "nsb", 0) + 1
        return TB(es.enter_context(self.nc.sbuf_tensor(f"s{self.nsb}_{name}", list(shape), dt)), name)

    def ps(self):
        assert self.psfree, "out of PSUM banks"
        i = self.psfree.pop(0)
        p = self.psb[i]
        p.idx = i
        return p

    def psf(self, *ps):
        for p in ps:
            self.psfree.append(p.idx)

    def _wait(self, eng, deps):
        if self.mute:
            return
        best = {}
        for (s, key, v) in deps:
            if key == "pe" and eng == "pe":
                continue
            if key not in best or v > best[key][1]:
                best[key] = (s, v)
        for key, (s, v) in best.items():
            if self.seen[eng].get(key, 0) < v:
                self.engs[eng].wait_ge(s, v)
                self.seen[eng][key] = v

    @staticmethod
    def _deps(R, W):
        deps = []
        for b in R:
            deps.extend(b.w)
        for b in W:
            deps.extend(b.w)
            deps.extend(b.r.values())
        return deps

    @staticmethod
    def _record(toks, R, W):
        for b in R:
            for tok in toks:
                old = b.r.get(tok[1])
                if old is None or old[2] < tok[2]:
                    b.r[tok[1]] = tok
        for b in W:
            b.w = list(toks)
            b.r = {}

    def op(self, eng, fn, R=(), W=()):
        if self.mute:
            return None
        self._wait(eng, self._deps(R, W))
        ins = fn()
        self.cnt[eng] += 1
        ins.then_inc(self.sem[eng], 1)
        tok = (self.sem[eng], eng, self.cnt[eng])
        self._record([tok], R, W)
        self.nins += 1
        return tok

    def dma(self, q, pairs, R=(), W=()):
        if self.mute:
            return []
        self._wait(q, self._deps(R, W))
        toks = []
        for (o, i) in pairs:
            ds = self.dsems[q][self.didx[q]]
            self.didx[q] = (self.didx[q] + 1) % len(self.dsems[q])
            if ds.count > 0 and self.seen[q].get(ds.key, 0) < ds.count:
                self.engs[q].wait_ge(ds.sem, ds.count)
                self.seen[q][ds.key] = ds.count
            ins = self.engs[q].dma_start(out=o, in_=i)
            ds.count += 16
            ins.then_inc(ds.sem, 16)
            toks.append((ds.sem, ds.key, ds.count))
            self.nins += 1
        self._record(toks, R, W)
        return toks

    def all_tokens(self):
        toks = [(self.sem[e], e, self.cnt[e]) for e in self.cnt if self.cnt[e] > 0]
        for q in self.dsems:
            toks += [(s.sem, s.key, s.count) for s in self.dsems[q] if s.count > 0]
        return toks

    def barrier(self):
        toks = self.all_tokens()
        for e in self.engs:
            self._wait(e, [t for t in toks if t[1] != e])

    def act(self, out, in_, func, R, W, bias=None, scale=None, accum=None):
        def fn():
            kw = {}
            if bias is not None:
                kw["bias"] = bias
            if scale is not None:
                kw["scale"] = scale
            if accum is not None:
                kw["accum_out"] = accum
            return self.nc.scalar.activation(out=out, in_=in_, func=func, **kw)
        return self.op("act", fn, R, W)

    def tt(self, out, in0, in1, op, R, W, eng="dve"):
        e = self.engs[eng]
        return self.op(eng, lambda: e.tensor_tensor(out=out, in0=in0, in1=in1, op=op), R, W)

    def ts(self, out, in0, s1, s2, op0, op1, R, W, eng="dve"):
        e = self.engs[eng]
        if op1 is None:
            return self.op(eng, lambda: e.tensor_scalar(out=out, in0=in0, scalar1=s1, scalar2=None, op0=op0), R, W)
        return self.op(eng, lambda: e.tensor_scalar(out=out, in0=in0, scalar1=s1, scalar2=s2, op0=op0, op1=op1), R, W)

    def stt(self, out, in0, scalar, in1, op0, op1, R, W):
        return self.op("dve", lambda: self.nc.vector.scalar_tensor_tensor(out=out, in0=in0, scalar=scalar, in1=in1,
                                                                           op0=op0, op1=op1), R, W)

    def recip(self, out, in_, R, W):
        return self.op("dve", lambda: self.nc.vector.reciprocal(out=out, in_=in_), R, W)

    def copy(self, eng, out, in_, R, W):
        if eng == "act":
            return self.op("act", lambda: self.nc.scalar.activation(out=out, in_=in_, func=AF.Copy), R, W)
        e = self.engs[eng]
        return self.op(eng, lambda: e.tensor_copy(out=out, in_=in_), R, W)

    def memset(self, eng, ap, val, W):
        e = self.engs[eng]
        return self.op(eng, lambda: e.memset(ap, val), [], W)

    def mm(self, items, R, W):
        def fn():
            ins = None
            for (o, l, r, st, sp) in items:
                ins = self.nc.tensor.matmul(o, lhsT=l, rhs=r, start=st, stop=sp)
            return ins
        return self.op("pe", fn, R, W)

    def tr(self, items, ident, R, W):
        def fn():
            ins = None
            for (o, i) in items:
                ins = self.nc.tensor.transpose(o, i, ident.t[:])
            return ins
        return self.op("pe", fn, list(R) + [ident.b], W)


class WRing:
    def __init__(self, kb, es, dram, sched):
        self.kb = kb
        self.d = dram
        self.slots = [kb.sb(f"wslot{i}", [128, 16, 512], BF16, es) for i in range(NSLOT)]
        for i, s in enumerate(self.slots):
            s.idx = i
        self.sched = sched
        self.rec = []
        self.free = list(range(NSLOT))
        self.slot_of = {}
        self.nfetch = 0
        self.nacq = 0

    def _fetch(self, i, spec):
        s = self.free.pop(0)
        self.slot_of[i] = s
        slot = self.slots[s]
        if spec[0][0].startswith("@"):
            (name, tile, nel) = spec[0]
            flat = slot.t[:].rearrange("p k c -> p (k c)")
            self.kb.dma("sp", [(flat[:, 0:nel], self.d[name[1:]][tile].rearrange("p k c -> p (k c)"))], R=[], W=[slot.b])
            return
        pairs = []
        for (name, r0, nr, c0, ncol, dc0) in spec:
            kc = nr // 128
            src = self.d[name][r0:r0 + nr, c0:c0 + ncol].rearrange("(k p) c -> p k c", p=128)
            pairs.append((slot.t[:, 0:kc, dc0:dc0 + ncol], src))
        self.kb.dma("pool", pairs, R=[], W=[slot.b])

    def prefetch(self):
        if self.sched is None or self.kb.mute:
            return
        while self.free and self.nfetch < len(self.sched):
            self._fetch(self.nfetch, self.sched[self.nfetch])
            self.nfetch += 1

    def acquire(self, spec):
        if self.kb.mute:
            return self.slots[0]
        spec = tuple(spec)
        i = self.nacq
        self.nacq += 1
        self.rec.append(spec)
        if self.sched is not None:
            assert self.sched[i] == spec, (i, self.sched[i], spec)
        if i >= self.nfetch:
            assert self.free, "weight ring full"
            self._fetch(i, spec)
            self.nfetch = i + 1
        return self.slots[self.slot_of[i]]

    def release(self, slot):
        if self.kb.mute:
            return
        self.free.append(slot.idx)
        self.prefetch()


class StopBuild(Exception):
    pass


STOP = [0]


class Builder:
    def ck(self, n):
        if STOP[0] == n:
            self.kb.barrier()
            self.kb.mute = True

    def __init__(self, dbg, sched):
        self.dbg = dbg
        self.nc = nc = bass.Bass("TRN2", target_bir_lowering=False)
        self.d = d = {}

        def din(name, shape):
            d[name] = nc.dram_tensor(name, list(shape), F32, kind="ExternalInput").ap()

        din("x_own", [T, D]); din("x_ctx", [T, D]); din("w_in", [D, INC]); din("w_lr", [D, 32])
        din("w_gate", [16, 2, 1024]); din("b_gate", [128, 16])
        din("norm_mix", [D]); din("norm_ffn", [D]); din("gla_norm", [512]); din("qk_gain", [128, 6])
        din("w_ba", [1024, D]); din("w_bg", [D, D]); din("w_out", [D, D]); din("w_ff1", [D, 8192]); din("w_ff2", [8192, D])
        din("cos_t", [32, 3072]); din("sin_t", [32, 3072]); din("amask", [128, 4, 256]); din("ident", [128, 128])
        din("perm", [32, 32]); din("trim", [128, 2, 128]); din("rmask", [128, 512])
        d["y"] = nc.dram_tensor("y", [T, D], F32, kind="ExternalOutput").ap()
        sk = "ExternalOutput" if dbg else "Internal"

        def dsc(name, shape, dt):
            d[name] = nc.dram_tensor(name, list(shape), dt, kind=sk).ap()

        dsc("oaT_d", [8, 128, T], BF16); dsc("ogT_d", [16, 128, T], BF16); dsc("mT_d", [16, 128, T], BF16)
        dsc("hnT_d", [16, 128, T], BF16); dsc("h_d", [T, D], F32); dsc("ob_d", [4, T, 512], F32)
        dsc("vsc_d", [4, 128, 16, 512], BF16); dsc("sctx_d", [8, 128, 512], F32); dsc("kh_d", [24, 128, 1024], BF16); dsc("vh_d", [24, 128, 1024], BF16)
        for nm, shp in (("c_m", [16, 128, 56, 128]), ("c_o", [4, 128, 16, 512]), ("c_f1", [16, 128, 16, 512]), ("c_f2", [16, 128, 16, 512])):
            d[nm] = nc.dram_tensor(nm, shp, BF16, kind="Internal").ap()
        jobs = []
        for fc in range(16):
            c0 = fc * 128
            jobs += [("c_m", fc, 0, "w_ba", 0, 1024, c0, 128), ("c_m", fc, 8, "w_bg", 0, D, c0, 128),
                     ("c_m", fc, 24, "w_in", 0, D, C_GA + c0, 128), ("c_m", fc, 40, "w_in", 0, D, C_GB + c0, 128)]
        for cb in range(4):
            jobs.append(("c_o", cb, 0, "w_out", 0, D, cb * 512, 512))
        for fg in range(16):
            jobs.append(("c_f1", fg, 0, "w_ff1", 0, D, fg * 512, 512))
        for cb in range(4):
            for kg in range(4):
                jobs.append(("c_f2", cb * 4 + kg, 0, "w_ff2", kg * 2048, 2048, cb * 512, 512))
        self.jobs = jobs
        self.njob = 0
        self.sched = sched

    def build(self):
        nc = self.nc
        with ExitStack() as es:
            self.kb = kb = KB(nc, es)
            self.W = WRing(kb, es, self.d, self.sched)
            self.consts(es)
            self.body()
            kb.mute = False
            kb._wait("sp", kb.all_tokens())
        return nc

    def body(self):
        kb = self.kb
        if True:
            self.ck(1)
            self.W.prefetch()
            with ExitStack() as es_c:
                xc = [kb.sb(f"xc{i}", [128, 16, 1024], BF16, es_c) for i in range(2)]
                with ExitStack() as es_s:
                    self.S = [[kb.sb(f"S{h}_{c}", [128, 512], F32, es_s) for c in range(2)] for h in range(4)]
                    for h in range(4):
                        for c in range(2):
                            kb.memset("dve", self.S[h][c].t[:], 0.0, [self.S[h][c].b])
                    for half in range(2):
                        self.norm_transpose(self.d["x_ctx"][half * 1024:(half + 1) * 1024, :], 8, xc[half].t, "norm_mix")
                        self.ck(2)
                        with ExitStack() as es_g:
                            g = self.gla_alloc(es_g, 1024, False)
                            self.lr_project(xc[half], 1024, g, (0,))
                            for h in range(4):
                                self.gla_sweep(h, xc[half], 1024, False, 0, "state", self.S[h], g)
                            kb.barrier()
                        self.ck(3)
                    for h in range(4):
                        for c in range(2):
                            kb.dma("sp", [(self.d["sctx_d"][h * 2 + c], self.S[h][c].t[:])], R=[self.S[h][c].b], W=[])
                    kb.barrier()
                    self.ck(4)
                with ExitStack() as es_a:
                    a = self.attn_alloc(es_a, halo_only=True)
                    for h in range(8):
                        for gi in range(3):
                            self.attn_halo(h, gi, xc[1], a)
                            if h == 0:
                                self.ck(41 + gi)
                    kb.barrier()
            self.ck(5)
            with ExitStack() as es_x:
                self.xn = kb.sb("xnT", [128, 16, T], BF16, es_x)
                self.norm_transpose(self.d["x_own"], 16, self.xn.t, "norm_mix")
                self.ck(6)
                with ExitStack() as es_a:
                    a = self.attn_alloc(es_a, halo_only=False)
                    for h in range(8):
                        for gi in range(3):
                            self.precache(3)
                            self.attn_head(h, gi, a)
                        self.attn_finish(h, a)
                    kb.barrier()
                self.ck(7)
                with ExitStack() as es_g:
                    g = self.gla_alloc(es_g, T, True)
                    self.lr_project(self.xn, T, g, (0, 1))
                    for h in range(4):
                        S = g["S"]
                        for c in range(2):
                            kb.dma("sp", [(S[c].t[:], self.d["sctx_d"][h * 2 + c])], R=[], W=[S[c].b])
                        self.precache(4)
                        self.obw = []
                        self.vsw = []
                        self.gla_sweep(h, self.xn, T, False, 0, "first", S, g)
                        self.precache(4)
                        for c in range(2):
                            kb.memset("dve", S[c].t[:], 0.0, [S[c].b])
                        self.gla_sweep(h, self.xn, T, True, 1, "final", S, g)
                    kb.barrier()
                self.precache(1000)
                kb.barrier()
                self.ck(8)
                self.merge_stage()
            self.ck(9)
            self.out_stage()
            self.ck(10)
            self.ffn_stage()

    def precache(self, n):
        for _ in range(n):
            if self.njob >= len(self.jobs):
                return
            (cn, tile, k0, sn, r0, nr, c0, ncol) = self.jobs[self.njob]
            self.njob += 1
            kc = nr // 128
            src = self.d[sn][r0:r0 + nr, c0:c0 + ncol].rearrange("(k p) c -> p k c", p=128)
            self.kb.dma("pc", [(self.d[cn][tile, :, k0:k0 + kc, :], src)])

    def consts(self, es):
        kb, d = self.kb, self.d
        self.ident = kb.sb("ident", [128, 128], BF16, es)
        kb.dma("pool", [(self.ident.t[:], d["ident"][:, :])], W=[self.ident.b])
        self.amask = kb.sb("amask", [128, 4, 256], BF16, es)
        kb.dma("pool", [(self.amask.t[:], d["amask"][:, :, :])], W=[self.amask.b])
        self.onesf = kb.sb("onesf", [128, 128], F32, es)
        kb.memset("dve", self.onesf.t[:], 1.0, [self.onesf.b])
        self.onesb = kb.sb("onesb", [128, 128], BF16, es)
        kb.memset("dve", self.onesb.t[:], 1.0, [self.onesb.b])
        self.perm = kb.sb("perm", [32, 32], F32, es)
        kb.dma("sp", [(self.perm.t[:], d["perm"][:, :])], W=[self.perm.b])
        self.permb = kb.sb("permb", [32, 32], BF16, es)
        kb.dma("pool", [(self.permb.t[:], d["perm"][:, :])], W=[self.permb.b])
        self.trim = kb.sb("trim", [128, 2, 128], F32, es)
        kb.dma("sp", [(self.trim.t[:], d["trim"][:, :, :])], W=[self.trim.b])
        self.rmask = kb.sb("rmask", [128, 512], F32, es)
        kb.dma("sp", [(self.rmask.t[:], d["rmask"][:, :])], W=[self.rmask.b])
        self.qkg = kb.sb("qkg", [128, 6], F32, es)
        kb.dma("sp", [(self.qkg.t[:], d["qk_gain"][:, :])], W=[self.qkg.b])
        self.wgate = kb.sb("wgate", [16, 2, 1024], F32, es)
        kb.dma("sp", [(self.wgate.t[:], d["w_gate"][:, :, :])], W=[self.wgate.b])
        self.nbg = kb.sb("nbg", [128, 16], F32, es)
        kb.dma("sp", [(self.nbg.t[:], d["b_gate"][:, :])], W=[self.nbg.b])
        kb.ts(self.nbg.t[:], self.nbg.t[:], -1.0, None, ALU.mult, None, R=[self.nbg.b], W=[self.nbg.b])
        self.cc = kb.sb("cc", [128, 4], F32, es)
        kb.memset("dve", self.cc.t[:, 0:1], EPS, [self.cc.b])
        kb.memset("dve", self.cc.t[:, 1:2], 1.0, [self.cc.b])
        kb.memset("dve", self.cc.t[:, 2:3], float(np.log(1.0 / 16.0)), [self.cc.b])
        kb.memset("dve", self.cc.t[:, 3:4], 0.0, [self.cc.b])
        kb.barrier()

    def rstd_from_ss(self, s, n):
        kb = self.kb
        kb.ts(s.t[:, 1:2], s.t[:, 0:1], 1.0 / n, EPS, ALU.mult, ALU.add, R=[s.b], W=[s.b])
        kb.act(s.t[:, 2:3], s.t[:, 1:2], AF.Sqrt, R=[s.b], W=[s.b])
        kb.recip(s.t[:, 3:4], s.t[:, 2:3], R=[s.b], W=[s.b])

    def norm_transpose(self, xd, ntiles, dst, gain_name, src_tiles=None):
        kb = self.kb
        with ExitStack() as es:
            xs = [kb.sb(f"n_x{i}", [128, D], F32, es) for i in range(2)]
            junk = kb.sb("n_junk", [128, D], BF16, es)
            xnb = [kb.sb(f"n_xnb{i}", [128, D], BF16, es) for i in range(2)]
            gbc = kb.sb("n_gbc", [128, D], F32, es)
            sm = [kb.sb(f"n_sm{i}", [128, 4], F32, es) for i in range(2)]
            kb.dma("sp", [(gbc.t[:], self.d[gain_name].partition_broadcast(128))], W=[gbc.b])
            for t in range(ntiles):
                x, s, xb = xs[t % 2], sm[t % 2], xnb[t % 2]
                kb.dma("sp", [(x.t[:], xd[t * 128:(t + 1) * 128, :])], W=[x.b])
                kb.act(junk.t[:], x.t[:], AF.Square, R=[x.b], W=[junk.b, s.b], accum=s.t[:, 0:1])
                self.rstd_from_ss(s, D)
                kb.stt(xb.t[:], x.t[:], s.t[:, 3:4], gbc.t[:], ALU.mult, ALU.mult, R=[x.b, s.b, gbc.b], W=[xb.b])
                for hh in range(2):
                    p = kb.ps()
                    pv = p.t[:].bitcast(BF16)
                    kb.tr([(pv[:, j * 128:(j + 1) * 128], xb.t[:, (hh * 8 + j) * 128:(hh * 8 + j + 1) * 128]) for j in range(8)],
                          self.ident, R=[xb.b], W=[p.b])
                    kb.copy("act" if hh == 0 else "dve", dst[:, hh * 8:(hh + 1) * 8, t * 128:(t + 1) * 128],
                            pv.rearrange("p (k t) -> p k t", k=8), R=[p.b], W=[])
                    kb.psf(p)
            kb.barrier()

    def gla_alloc(self, es, ntok, full):
        kb = self.kb
        g = {}
        g["lrT"] = [kb.sb(f"g_lrT{i}", [16, ntok], F32, es) for i in range(2 if full else 1)]
        for nm in ("L", "P", "E", "Dd", "X0", "X1"):
            g[nm] = kb.sb("g_" + nm, [128, 512], F32, es)
        g["tot"] = kb.sb("g_tot", [128, 4], F32, es)
        g["edec"] = kb.sb("g_edec", [128, 2, 4], F32, es)
        g["qt"] = [kb.sb(f"g_qt{c}", [128, 512], BF16, es) for c in range(2)]
        g["kt"] = [kb.sb(f"g_kt{c}", [128, 512], BF16, es) for c in range(2)]
        g["khT"] = kb.sb("g_khT", [128, 512], BF16, es)
        g["khat"] = kb.sb("g_khat", [128, 4, 256], BF16, es)
        g["vts"] = [kb.sb(f"g_vt{i}", [128, 4, 512], BF16, es) for i in range(2)]
        g["Sbf"] = [[kb.sb(f"g_Sbf{i}_{c}", [128, 512], BF16, es) for c in range(2)] for i in range(2)]
        g["S2"] = [kb.sb(f"g_S2_{c}", [128, 512], F32, es) for c in range(2)]
        if full:
            g["S"] = [kb.sb(f"g_S{c}", [128, 512], F32, es) for c in range(2)]
            g["attT"] = kb.sb("g_attT", [128, 128], BF16, es)
            g["ot"] = [kb.sb(f"g_ot{i}", [128, 512], F32, es) for i in range(4)]
            g["sm4"] = [kb.sb(f"g_sm4_{i}", [128, 4], F32, es) for i in range(4)]
            g["on"] = g["X0"]
            g["sg4"] = [kb.sb(f"g_sg{i}", [128, 512], F32, es) for i in range(4)]
            g["og"] = kb.sb("g_og", [128, 512], BF16, es)
            g["ogT"] = kb.sb("g_ogT", [128, 4, 512], BF16, es)
            g["gnbc"] = kb.sb("g_gnbc", [128, 512], F32, es)
            g["sm"] = kb.sb("g_sm", [128, 4], F32, es)
            g["junk"] = g["og"]
            kb.dma("sp", [(g["gnbc"].t[:], self.d["gla_norm"].partition_broadcast(128))], W=[g["gnbc"].b])
        return g

    def lr_project(self, xn, ntok, g, dirs):
        kb = self.kb
        for dirn in dirs:
            w = self.W.acquire([("w_lr", 0, D, dirn * 16, 16, 0)])
            for blk in range(ntok // 512):
                p = kb.ps()
                kb.mm([(p.t[0:16, :], w.t[:, kc, 0:16], xn.t[:, kc, blk * 512:(blk + 1) * 512], kc == 0, kc == 15)
                       for kc in range(16)], R=[w.b, xn.b], W=[p.b])
                kb.copy("act", g["lrT"][dirn].t[0:16, blk * 512:(blk + 1) * 512], p.t[0:16, :], R=[p.b], W=[g["lrT"][dirn].b])
                kb.psf(p)
            self.W.release(w)

    def gla_sweep(self, hd, xn, ntok, rev, dirn, mode, S, g):
        kb, W, d = self.kb, self.W, self.d
        outp = mode != "state"
        if outp:
            wqk = W.acquire([("w_in", 0, D, C_QG + hd * 256, 256, 0), ("w_in", 0, D, C_KG + hd * 256, 256, 256)])
        else:
            wqk = W.acquire([("w_in", 0, D, C_KG + hd * 256, 256, 256)])
        wv = W.acquire([("w_in", 0, D, C_VG + hd * 512, 512, 0)]) if mode != "final" else None
        wrg = W.acquire([("w_in", 0, D, C_RG + hd * 512, 512, 0)]) if mode == "final" else None
        lrT = g["lrT"][dirn]
        L, P, E, Dd, X0, X1 = g["L"], g["P"], g["E"], g["Dd"], g["X0"], g["X1"]
        tot, edec = g["tot"], g["edec"]
        Sbfs = g["Sbf"]
        Scur, Salt = [S[0], S[1]], [g["S2"][0], g["S2"][1]]
        sbi = 0
        nblk = ntok // 512
        if outp:
            for c in range(2):
                kb.copy("act", Sbfs[sbi][c].t[:], Scur[c].t[:], R=[Scur[c].b], W=[Sbfs[sbi][c].b])
        for bi in range(nblk):
            blk = (nblk - 1 - bi) if rev else bi
            t0 = blk * 512
            g["vt"] = g["vts"][bi % 2]
            if mode == "final":
                kb._wait("sp", self.vsw)
                kb.dma("sp", [(g["vt"].t[:], d["vsc_d"][hd, :, blk * 4:(blk + 1) * 4, :])], R=[], W=[g["vt"].b])
            pq = []
            if outp:
                for c in range(2):
                    p = kb.ps()
                    kb.mm([(p.t[:, :], wqk.t[:, kc, c * 128:(c + 1) * 128], xn.t[:, kc, t0:t0 + 512], kc == 0, kc == 15)
                           for kc in range(16)], R=[wqk.b, xn.b], W=[p.b])
                    pq.append(p)
            pk = []
            for c in range(2):
                p = kb.ps()
                kb.mm([(p.t[:, :], wqk.t[:, kc, 256 + c * 128:256 + (c + 1) * 128], xn.t[:, kc, t0:t0 + 512], kc == 0, kc == 15)
                       for kc in range(16)], R=[wqk.b, xn.b], W=[p.b])
                pk.append(p)
            pzs = []
            for c in range(2):
                pz = kb.ps()
                kb.mm([(pz.t[:, :], self.wgate.t[0:16, dirn, hd * 256 + c * 128:hd * 256 + (c + 1) * 128],
                        lrT.t[0:16, t0:t0 + 512], True, True)], R=[self.wgate.b, lrT.b], W=[pz.b])
                pzs.append(pz)
            def vmm(tl):
                p = kb.ps()
                kb.mm([(p.t[:, :], xn.t[:, kc, t0 + tl * 128:t0 + (tl + 1) * 128], wv.t[:, kc, :], kc == 0, kc == 15)
                       for kc in range(16)], R=[wv.b, xn.b], W=[p.b])
                return p

            def vcp(tl, p):
                kb.copy("act", g["vt"].t[:, tl, :], p.t[:, :], R=[p.b], W=[g["vt"].b])
                kb.psf(p)

            def rgmm(tl_):
                tok_ = t0 + tl_ * 128
                pr_ = kb.ps()
                kb.mm([(pr_.t[:, :], xn.t[:, kc, tok_:tok_ + 128], wrg.t[:, kc, :], kc == 0, kc == 15) for kc in range(16)],
                      R=[wrg.b, xn.b], W=[pr_.b])
                return pr_

            for c in range(2):
                pv2 = [vmm(2 * c), vmm(2 * c + 1)] if mode != "final" else None
                prs = [rgmm(2 * c), rgmm(2 * c + 1)] if mode == "final" else None
                pz = pzs[c]
                col = dirn * 8 + hd * 2 + c
                kb.act(X0.t[:], pz.t[:], AF.Exp, R=[pz.b, self.nbg.b], W=[X0.b], bias=self.nbg.t[:, col:col + 1], scale=-1.0)
                kb.psf(pz)
                kb.act(L.t[:], X0.t[:], AF.Ln, R=[X0.b, self.cc.b], W=[L.b], bias=self.cc.t[:, 1:2])
                kb.op("dve", lambda: self.nc.vector.tensor_tensor_scan(out=P.t[:], data0=self.rmask.t[:], data1=L.t[:], initial=0.0,
                                                                       op0=ALU.mult, op1=ALU.add), R=[self.rmask.b, L.b], W=[P.b])
                P3 = P.t[:].rearrange("p (t i) -> p t i", i=128)
                kb.copy("dve", tot.t[:].unsqueeze(2), P3[:, :, 127:128], R=[P.b], W=[tot.b])
                totb = tot.t[:].unsqueeze(2).broadcast_to([128, 4, 128])
                E3 = E.t[:].rearrange("p (t i) -> p t i", i=128)
                D3 = Dd.t[:].rearrange("p (t i) -> p t i", i=128)
                if rev:
                    kb.tt(Dd.t[:], P.t[:], L.t[:], ALU.subtract, R=[P.b, L.b], W=[Dd.b])
                    kb.tt(E3, totb, D3, ALU.subtract, R=[tot.b, Dd.b], W=[E.b])
                    Eb = E
                else:
                    kb.tt(D3, totb, P3, ALU.subtract, R=[tot.b, P.b], W=[Dd.b])
                    Eb = P
                kb.act(edec.t[:, c, :], tot.t[:], AF.Exp, R=[tot.b], W=[edec.b], scale=-1.0 / 16.0)
                if outp:
                    kb.act(X0.t[:], Eb.t[:], AF.Exp, R=[Eb.b, self.cc.b], W=[X0.b], bias=self.cc.t[:, 2:3], scale=-1.0 / 16.0)
                    kb.tt(g["qt"][c].t[:], pq[c].t[:], X0.t[:], ALU.mult, R=[pq[c].b, X0.b], W=[g["qt"][c].b])
                    kb.act(X1.t[:], Eb.t[:], AF.Exp, R=[Eb.b], W=[X1.b], scale=1.0 / 16.0)
                    kb.tt(g["kt"][c].t[:], pk[c].t[:], X1.t[:], ALU.mult, R=[pk[c].b, X1.b], W=[g["kt"][c].b])
                kb.act(X0.t[:], Dd.t[:], AF.Exp, R=[Dd.b], W=[X0.b], scale=-1.0 / 16.0)
                kb.tt(g["khT"].t[:], pk[c].t[:], X0.t[:], ALU.mult, R=[pk[c].b, X0.b], W=[g["khT"].b])
                pt = kb.ps()
                ptv = pt.t[:].bitcast(BF16)
                kb.tr([(ptv[:, tl * 128:(tl + 1) * 128], g["khT"].t[:, tl * 128:(tl + 1) * 128]) for tl in range(4)],
                      self.ident, R=[g["khT"].b], W=[pt.b])
                kb.copy("act", g["khat"].t[:, :, c * 128:(c + 1) * 128], ptv[:, 0:512].rearrange("p (t k) -> p t k", t=4),
                        R=[pt.b], W=[g["khat"].b])
                kb.psf(pt)
                if pv2 is not None:
                    vcp(2 * c, pv2[0])
                    vcp(2 * c + 1, pv2[1])
                if prs is not None:
                    for k2 in range(2):
                        sgb = g["sg4"][2 * c + k2]
                        kb.act(sgb.t[:], prs[k2].t[:, :], AF.Silu, R=[prs[k2].b], W=[sgb.b])
                        kb.psf(prs[k2])
            if mode == "first":
                self.vsw.extend(kb.dma("sp", [(d["vsc_d"][hd, :, blk * 4:(blk + 1) * 4, :], g["vt"].t[:])], R=[g["vt"].b], W=[]))
            if outp:
                kb.psf(*pq)
            kb.psf(*pk)
            if mode == "final":
                kb._wait("sp", self.obw)
                for ti in range(4):
                    tl = (3 - ti) if rev else ti
                    tok = t0 + tl * 128
                    kb.dma("sp", [(g["ot"][tl].t[:], d["ob_d"][hd, tok:tok + 128, :])], R=[], W=[g["ot"][tl].b])
            for ti in range(4):
                tl = (3 - ti) if rev else ti
                tok = t0 + tl * 128
                sl = slice(tl * 128, (tl + 1) * 128)
                Sbf = Sbfs[sbi]
                if outp:
                    ps_s = kb.ps()
                    kb.mm([(ps_s.t[:, 0:128], g["kt"][c].t[:, sl], g["qt"][c].t[:, sl], c == 0, c == 1) for c in range(2)],
                          R=[g["kt"][0].b, g["kt"][1].b, g["qt"][0].b, g["qt"][1].b], W=[ps_s.b])
                    kb.tt(g["attT"].t[:], ps_s.t[:, 0:128], self.trim.t[:, 1 if rev else 0, :], ALU.mult,
                          R=[ps_s.b, self.trim.b], W=[g["attT"].b])
                    kb.psf(ps_s)
                psts = []
                for c in range(2):
                    pst = kb.ps()
                    kb.mm([(pst.t[:, :], g["khat"].t[:, tl, c * 128:(c + 1) * 128], g["vt"].t[:, tl, :], True, True)],
                          R=[g["khat"].b, g["vt"].b], W=[pst.b])
                    psts.append(pst)
                if outp:
                    ps_o = kb.ps()
                    kb.mm([(ps_o.t[:, :], g["attT"].t[:], g["vt"].t[:, tl, :], True, False),
                           (ps_o.t[:, :], g["qt"][0].t[:, sl], Sbf[0].t[:], False, False),
                           (ps_o.t[:, :], g["qt"][1].t[:, sl], Sbf[1].t[:], False, True)],
                          R=[g["attT"].b, g["vt"].b, g["qt"][0].b, g["qt"][1].b, Sbf[0].b, Sbf[1].b], W=[ps_o.b])
                for c in range(2):
                    kb.stt(Salt[c].t[:], Scur[c].t[:], edec.t[:, c, tl:tl + 1], psts[c].t[:, :], ALU.mult, ALU.add,
                           R=[Scur[c].b, edec.b, psts[c].b], W=[Salt[c].b])
                    kb.psf(psts[c])
                    if outp:
                        kb.copy("act", Sbfs[1 - sbi][c].t[:], Salt[c].t[:], R=[Salt[c].b], W=[Sbfs[1 - sbi][c].b])
                Scur, Salt = Salt, Scur
                sbi = 1 - sbi
                if mode == "first":
                    ot = g["ot"][ti % 2]
                    kb.copy("act", ot.t[:], ps_o.t[:, :], R=[ps_o.b], W=[ot.b])
                    kb.psf(ps_o)
                    self.obw.extend(kb.dma("sp", [(d["ob_d"][hd, tok:tok + 128, :], ot.t[:])], R=[ot.b], W=[]))
                elif mode == "final":
                    ot = g["ot"][tl]
                    kb.tt(ot.t[:], ps_o.t[:, :], ot.t[:], ALU.add, R=[ps_o.b, ot.b], W=[ot.b])
                    kb.psf(ps_o)
            if mode == "final":
                for tl in range(4):
                    sm = g["sm4"][tl]
                    kb.act(g["junk"].t[:], g["ot"][tl].t[:], AF.Square, R=[g["ot"][tl].b], W=[g["junk"].b, sm.b], accum=sm.t[:, 0:1])
                for tl in range(4):
                    sm = g["sm4"][tl]
                    kb.ts(sm.t[:, 1:2], sm.t[:, 0:1], 1.0 / 512, EPS, ALU.mult, ALU.add, R=[sm.b], W=[sm.b])
                    kb.act(sm.t[:, 2:3], sm.t[:, 1:2], AF.Ln, R=[sm.b], W=[sm.b])
                    kb.act(sm.t[:, 3:4], sm.t[:, 2:3], AF.Exp, R=[sm.b], W=[sm.b], scale=-0.5)
                for tl in range(4):
                    sl = slice(tl * 128, (tl + 1) * 128)
                    sm = g["sm4"][tl]
                    ot = g["ot"][tl]
                    kb.stt(g["on"].t[:], ot.t[:], sm.t[:, 3:4], g["gnbc"].t[:], ALU.mult, ALU.mult,
                           R=[ot.b, sm.b, g["gnbc"].b], W=[g["on"].b])
                    kb.tt(g["og"].t[:], g["on"].t[:], g["sg4"][tl].t[:], ALU.mult, R=[g["on"].b, g["sg4"][tl].b], W=[g["og"].b])
                    pt = kb.ps()
                    ptv = pt.t[:].bitcast(BF16)
                    kb.tr([(ptv[:, c4 * 128:(c4 + 1) * 128], g["og"].t[:, c4 * 128:(c4 + 1) * 128]) for c4 in range(4)],
                          self.ident, R=[g["og"].b], W=[pt.b])
                    kb.copy("dve", g["ogT"].t[:, :, sl], ptv[:, 0:512].rearrange("p (c t) -> p c t", c=4), R=[pt.b], W=[g["ogT"].b])
                    kb.psf(pt)
                kb.dma("sp", [(d["ogT_d"][hd * 4:(hd + 1) * 4, :, t0:t0 + 512].rearrange("c p t -> p c t"), g["ogT"].t[:])],
                       R=[g["ogT"].b], W=[])
        assert Scur[0] is S[0]
        W.release(wqk)
        if wv is not None:
            W.release(wv)
        if wrg is not None:
            W.release(wrg)

    def attn_alloc(self, es, halo_only):
        kb = self.kb
        a = {}
        n = 1024 if halo_only else 4096
        a["kT"] = kb.sb("a_kT", [128, n], BF16, es)
        a["vT"] = kb.sb("a_vT", [128, n], BF16, es)
        a["sq"] = kb.sb("a_sq", [128, 512], BF16, es)
        a["xg"] = kb.sb("a_xg", [128, 512], F32, es)
        a["rt"] = kb.sb("a_rt", [128, 512], F32, es)
        a["xgb"] = kb.sb("a_xgb", [32, 512], BF16, es)
        a["t1"] = kb.sb("a_t1", [32, 512], F32, es)
        a["t2"] = kb.sb("a_t2", [32, 512], F32, es)
        a["cs"] = [kb.sb(f"a_cs{i}", [32, 512], F32, es) for i in range(2)]
        a["sn"] = [kb.sb(f"a_sn{i}", [32, 512], F32, es) for i in range(2)]
        a["tabi"] = 0
        if not halo_only:
            a["qT"] = kb.sb("a_qT", [128, T], BF16, es)
            a["V"] = kb.sb("a_V", [128, 32, 128], BF16, es)
            a["acc"] = kb.sb("a_acc", [128, 2, T], F32, es)
            a["pT"] = [kb.sb(f"a_pT{i}", [128, 256], BF16, es) for i in range(2)]
            a["pm"] = [kb.sb(f"a_pm{i}", [128, 256], BF16, es) for i in range(2)]
            a["oo"] = kb.sb("a_oo", [128, T], BF16, es)
            a["qi"] = 0
        return a

    def norm_rope(self, praw, n, gcol, tab_off, out_ap, out_b, a):
        kb = self.kb
        sq, xg, rt, t1, t2 = a["sq"], a["xg"], a["rt"], a["t1"], a["t2"]
        i = a["tabi"]
        a["tabi"] = 1 - i
        cs, sn = a["cs"][i], a["sn"][i]
        kb.dma("sp", [(cs.t[:, 0:n], self.d["cos_t"][:, tab_off:tab_off + n])], W=[cs.b])
        kb.dma("sp", [(sn.t[:, 0:n], self.d["sin_t"][:, tab_off:tab_off + n])], W=[sn.b])
        kb.act(sq.t[:, 0:n], praw.t[:, 0:n], AF.Square, R=[praw.b], W=[sq.b])
        kb.act(xg.t[:, 0:n], praw.t[:, 0:n], AF.Copy, R=[praw.b, self.qkg.b], W=[xg.b], scale=self.qkg.t[:, gcol:gcol + 1])
        xgb = a["xgb"]
        kb.act(xgb.t[:, 0:n], praw.t[0:32, 0:n], AF.Copy, R=[praw.b, self.qkg.b], W=[xgb.b], scale=self.qkg.t[0:32, gcol:gcol + 1])
        psum_ = kb.ps()
        kb.mm([(psum_.t[:, 0:n], self.onesb.t[:], sq.t[:, 0:n], True, True)], R=[self.onesb.b, sq.b], W=[psum_.b])
        psw = kb.ps()
        kb.mm([(psw.t[0:32, 0:n], self.permb.t[0:32, 0:32], xgb.t[0:32, 0:n], True, True)], R=[self.permb.b, xgb.b], W=[psw.b])
        kb.act(rt.t[:, 0:n], psum_.t[:, 0:n], AF.Ln, R=[psum_.b, self.cc.b], W=[rt.b], bias=self.cc.t[:, 0:1], scale=1.0 / 128.0)
        kb.psf(psum_)
        kb.act(rt.t[:, 0:n], rt.t[:, 0:n], AF.Exp, R=[rt.b], W=[rt.b], scale=-0.5)
        kb.tt(t1.t[:, 0:n], xg.t[0:32, 0:n], cs.t[:, 0:n], ALU.mult, R=[xg.b, cs.b], W=[t1.b])
        kb.tt(t2.t[:, 0:n], psw.t[0:32, 0:n], sn.t[:, 0:n], ALU.mult, R=[psw.b, sn.b], W=[t2.b])
        kb.psf(psw)
        kb.tt(xg.t[0:32, 0:n], t1.t[:, 0:n], t2.t[:, 0:n], ALU.add, R=[t1.b, t2.b], W=[xg.b])
        kb.tt(out_ap, xg.t[:, 0:n], rt.t[:, 0:n], ALU.mult, R=[xg.b, rt.b], W=[out_b])

    def attn_w(self, h, gi, with_q):
        col = gi * 1024 + h * 128
        spec = []
        if with_q:
            spec.append(("w_in", 0, D, C_QA + col, 128, 0))
        spec.append(("w_in", 0, D, C_KA + col, 128, 128))
        spec.append(("w_in", 0, D, C_VA + col, 128, 256))
        return self.W.acquire(spec)

    def attn_halo(self, h, gi, xc, a):
        kb = self.kb
        r = DIL[gi]
        halo = 64 * r
        w = self.attn_w(h, gi, False)
        idx = gi * 8 + h
        done = 0
        while done < halo:
            n = min(512, halo - done)
            x0 = 1024 - halo + done
            p = kb.ps()
            kb.mm([(p.t[:, 0:n], w.t[:, kc, 128:256], xc.t[:, kc, x0:x0 + n], kc == 0, kc == 15) for kc in range(16)],
                  R=[w.b, xc.b], W=[p.b])
            self.norm_rope(p, n, 3 + gi, x0, a["kT"].t[:, done:done + n], a["kT"].b, a)
            kb.psf(p)
            p = kb.ps()
            kb.mm([(p.t[:, 0:n], w.t[:, kc, 256:384], xc.t[:, kc, x0:x0 + n], kc == 0, kc == 15) for kc in range(16)],
                  R=[w.b, xc.b], W=[p.b])
            kb.copy("act", a["vT"].t[:, done:done + n], p.t[:, 0:n], R=[p.b], W=[a["vT"].b])
            kb.psf(p)
            done += n
        self.W.release(w)
        kb.dma("sp", [(self.d["kh_d"][idx, :, 0:halo], a["kT"].t[:, 0:halo])], R=[a["kT"].b], W=[])
        kb.dma("sp", [(self.d["vh_d"][idx, :, 0:halo], a["vT"].t[:, 0:halo])], R=[a["vT"].b], W=[])

    def attn_head(self, h, gi, a):
        kb = self.kb
        r = DIL[gi]
        halo = 64 * r
        Wn = T + 128 * r
        idx = gi * 8 + h
        kT, vT, qT, V, acc = a["kT"], a["vT"], a["qT"], a["V"], a["acc"]
        kb.memset("dve", kT.t[:, halo + T:Wn], 0.0, [kT.b])
        kb.memset("dve", vT.t[:, halo + T:Wn], 0.0, [vT.b])
        kb.dma("sp", [(kT.t[:, 0:halo], self.d["kh_d"][idx, :, 0:halo])], W=[kT.b])
        kb.dma("sp", [(vT.t[:, 0:halo], self.d["vh_d"][idx, :, 0:halo])], W=[vT.b])
        w = self.attn_w(h, gi, True)
        items = [(blk, which) for blk in range(4) for which in range(3)]

        def proj(it):
            blk, which = it
            t0 = blk * 512
            p = kb.ps()
            kb.mm([(p.t[:, :], w.t[:, kc, which * 128:(which + 1) * 128], self.xn.t[:, kc, t0:t0 + 512], kc == 0, kc == 15)
                   for kc in range(16)], R=[w.b, self.xn.b], W=[p.b])
            return p

        def post(it, p):
            blk, which = it
            t0 = blk * 512
            if which == 0:
                self.norm_rope(p, 512, gi, 1024 + t0, qT.t[:, t0:t0 + 512], qT.b, a)
            elif which == 1:
                self.norm_rope(p, 512, 3 + gi, 1024 + t0, kT.t[:, halo + t0:halo + t0 + 512], kT.b, a)
            else:
                kb.copy("act", vT.t[:, halo + t0:halo + t0 + 512], p.t[:, :], R=[p.b], W=[vT.b])
            kb.psf(p)

        pcur = proj(items[0])
        for ii, it in enumerate(items):
            pnext = proj(items[ii + 1]) if ii + 1 < len(items) else None
            post(it, pcur)
            pcur = pnext
        self.W.release(w)
        nq = T // (128 * r)
        vi = 0
        for c in range(r):
            for j in range(nq):
                p = kb.ps()
                pv = p.t[:].bitcast(BF16)
                base = c + r * 128 * j
                inA = [vT.t[:, base + r * 64 * bb: base + r * 64 * bb + r * 63 + 1: r] for bb in (0, 3)]
                inB = vT.t[:, base + r * 64: base + r * 64 + r * 127 + 1: r]
                kb.tr([(pv[0:64, 0:128], inA[0]), (pv[64:128, 0:128], inA[1]), (pv[:, 128:256], inB)],
                      self.ident, R=[vT.b], W=[p.b])
                kb.copy("act" if (vi % 2 == 0) else "dve", V.t[:, 2 * vi:2 * vi + 2, :], pv[:, 0:256].rearrange("p (a d) -> p a d", a=2),
                        R=[p.b], W=[V.b])
                kb.psf(p)
                vi += 1
        sc = float(128.0 ** -0.5)
        tiles = [(c, j) for c in range(r) for j in range(nq)]

        def scores(vi):
            c, j = tiles[vi]
            first, last = (j == 0), (j == nq - 1)
            mi = 3 if (first and last) else (0 if first else (2 if last else 1))
            base = c + r * 128 * j
            qs = qT.t[:, base: base + r * 127 + 1: r]
            ps_s = kb.ps()
            kA = [kT.t[:, base + r * 64 * bb: base + r * 64 * bb + r * 63 + 1: r] for bb in (0, 3)]
            kB = kT.t[:, base + r * 64: base + r * 64 + r * 127 + 1: r]
            kb.mm([(ps_s.t[0:64, 0:128], kA[0], qs, True, True), (ps_s.t[64:128, 0:128], kA[1], qs, True, True),
                   (ps_s.t[:, 128:256], kB, qs, True, True)], R=[kT.b, qT.b], W=[ps_s.b])
            i = a["qi"]
            a["qi"] = 1 - i
            pT, pm = a["pT"][i], a["pm"][i]
            kb.act(pT.t[:], ps_s.t[:, 0:256], AF.Exp, R=[ps_s.b], W=[pT.b], scale=sc)
            kb.psf(ps_s)
            kb.tt(pm.t[:], pT.t[:], self.amask.t[:, mi, :], ALU.mult, R=[pT.b, self.amask.b], W=[pm.b])
            return pm

        def pv(vi, pm):
            c, j = tiles[vi]
            base = c + r * 128 * j
            ps_o = kb.ps()
            kb.mm([(ps_o.t[:, 0:128], V.t[:, 2 * vi, :], pm.t[:, 0:128], True, False),
                   (ps_o.t[:, 0:128], V.t[:, 2 * vi + 1, :], pm.t[:, 128:256], False, True),
                   (ps_o.t[:, 128:256], self.onesb.t[:], pm.t[:, 0:128], True, False),
                   (ps_o.t[:, 128:256], self.onesb.t[:], pm.t[:, 128:256], False, True)],
                  R=[V.b, pm.b, self.onesb.b], W=[ps_o.b])
            dst = acc.t[:, :, base: base + r * 127 + 1: r]
            src = ps_o.t[:, 0:256].rearrange("p (a q) -> p a q", a=2)
            if gi == 0:
                kb.copy("act", dst, src, R=[ps_o.b], W=[acc.b])
            else:
                kb.tt(dst, dst, src, ALU.add, R=[ps_o.b, acc.b], W=[acc.b])
            kb.psf(ps_o)

        prev = None
        for vi in range(len(tiles)):
            pm = scores(vi)
            if prev is not None:
                pv(*prev)
            prev = (vi, pm)
        pv(*prev)

    def attn_finish(self, h, a):
        kb = self.kb
        acc, oo = a["acc"], a["oo"]
        kb.act(acc.t[:, 1, :], acc.t[:, 1, :], AF.Ln, R=[acc.b], W=[acc.b])
        kb.act(acc.t[:, 1, :], acc.t[:, 1, :], AF.Exp, R=[acc.b], W=[acc.b], scale=-1.0)
        kb.tt(oo.t[:], acc.t[:, 0, :], acc.t[:, 1, :], ALU.mult, R=[acc.b], W=[oo.b])
        kb.dma("sp", [(self.d["oaT_d"][h], oo.t[:])], R=[oo.b], W=[])

    def merge_stage(self):
        kb, d, W = self.kb, self.d, self.W
        with ExitStack() as es:
            oas = [kb.sb(f"m_oa{i}", [128, 8, 512], BF16, es) for i in range(2)]
            ogs = [kb.sb(f"m_og{i}", [128, 16, 512], BF16, es) for i in range(2)]

            def mload(blk_):
                t0_ = blk_ * 512
                kb.dma("sp", [(oas[blk_ % 2].t[:], d["oaT_d"][:, :, t0_:t0_ + 512].rearrange("c p t -> p c t"))], W=[oas[blk_ % 2].b])
                kb.dma("sp", [(ogs[blk_ % 2].t[:], d["ogT_d"][:, :, t0_:t0_ + 512].rearrange("c p t -> p c t"))], W=[ogs[blk_ % 2].b])

            mload(0)
            mT = kb.sb("m_mT", [128, 16, 512], BF16, es)
            sa = kb.sb("m_sa", [128, 512], F32, es)
            sb_ = kb.sb("m_sb", [128, 512], F32, es)
            ta = kb.sb("m_ta", [128, 512], F32, es)
            tb_ = kb.sb("m_tb", [128, 512], F32, es)
            for blk in range(4):
                t0 = blk * 512
                oa, og = oas[blk % 2], ogs[blk % 2]
                for fc in range(16):
                    c0 = fc * 128
                    if fc == 4 and blk + 1 < 4:
                        mload(blk + 1)
                    w = W.acquire([("@c_m", fc, 56 * 128)])
                    wv = w.t[:].rearrange("p k (a c) -> p (k a) c", c=128)
                    pua, pub, pga, pgb = kb.ps(), kb.ps(), kb.ps(), kb.ps()
                    kb.mm([(pua.t[:, :], wv[:, kc, :], oa.t[:, kc, :], kc == 0, kc == 7) for kc in range(8)], R=[w.b, oa.b], W=[pua.b])
                    kb.mm([(pub.t[:, :], wv[:, 8 + kc, :], og.t[:, kc, :], kc == 0, kc == 15) for kc in range(16)], R=[w.b, og.b], W=[pub.b])
                    kb.mm([(pga.t[:, :], wv[:, 24 + kc, :], self.xn.t[:, kc, t0:t0 + 512], kc == 0, kc == 15) for kc in range(16)],
                          R=[w.b, self.xn.b], W=[pga.b])
                    kb.mm([(pgb.t[:, :], wv[:, 40 + kc, :], self.xn.t[:, kc, t0:t0 + 512], kc == 0, kc == 15) for kc in range(16)],
                          R=[w.b, self.xn.b], W=[pgb.b])
                    W.release(w)
                    kb.act(sa.t[:], pga.t[:, :], AF.Sigmoid, R=[pga.b], W=[sa.b])
                    kb.act(sb_.t[:], pgb.t[:, :], AF.Sigmoid, R=[pgb.b], W=[sb_.b])
                    kb.tt(ta.t[:], sa.t[:], pua.t[:, :], ALU.mult, R=[sa.b, pua.b], W=[ta.b])
                    kb.tt(tb_.t[:], sb_.t[:], pub.t[:, :], ALU.mult, R=[sb_.b, pub.b], W=[tb_.b])
                    kb.psf(pua, pub, pga, pgb)
                    kb.tt(mT.t[:, fc, :], ta.t[:], tb_.t[:], ALU.add, R=[ta.b, tb_.b], W=[mT.b])
                kb.dma("sp", [(d["mT_d"][:, :, t0:t0 + 512].rearrange("c p t -> p c t"), mT.t[:])], R=[mT.b], W=[])
            kb.barrier()

    def out_stage(self):
        kb, d, W = self.kb, self.d, self.W
        with ExitStack() as es:
            mTs = [kb.sb(f"o_mT{i}", [128, 16, 512], BF16, es) for i in range(2)]
            hss = [[kb.sb(f"o_h{j}_{i}", [128, D], F32, es) for i in range(4)] for j in range(2)]
            junk = kb.sb("o_junk", [128, D], BF16, es)
            hn = kb.sb("o_hn", [128, D], BF16, es)
            hnT = kb.sb("o_hnT", [128, 16, 128], BF16, es)
            gbc = kb.sb("o_gbc", [128, D], F32, es)
            sm = kb.sb("o_sm", [128, 4], F32, es)
            kb.dma("sp", [(gbc.t[:], d["norm_ffn"].partition_broadcast(128))], W=[gbc.b])

            def mmblk(blk):
                t0 = blk * 512
                mT, hs = mTs[blk % 2], hss[blk % 2]
                kb.dma("sp", [(mT.t[:], d["mT_d"][:, :, t0:t0 + 512].rearrange("c p t -> p c t"))], W=[mT.b])
                for tl in range(4):
                    kb.dma("sp", [(hs[tl].t[:], d["x_own"][t0 + tl * 128:t0 + (tl + 1) * 128, :])], W=[hs[tl].b])
                for cb in range(4):
                    w = W.acquire([("@c_o", cb, 8192)])
                    for tl in range(4):
                        p = kb.ps()
                        kb.mm([(p.t[:, :], mT.t[:, kc, tl * 128:(tl + 1) * 128], w.t[:, kc, :], kc == 0, kc == 15) for kc in range(16)],
                              R=[w.b, mT.b], W=[p.b])
                        hv = hs[tl].t[:, cb * 512:(cb + 1) * 512]
                        kb.tt(hv, hv, p.t[:, :], ALU.add, R=[p.b, hs[tl].b], W=[hs[tl].b])
                        kb.psf(p)
                    W.release(w)

            def postblk(blk):
                t0 = blk * 512
                hs = hss[blk % 2]
                for tl in range(4):
                    h = hs[tl]
                    tok = t0 + tl * 128
                    kb.dma("sp", [(d["h_d"][tok:tok + 128, :], h.t[:])], R=[h.b], W=[])
                    kb.act(junk.t[:], h.t[:], AF.Square, R=[h.b], W=[junk.b, sm.b], accum=sm.t[:, 0:1])
                    self.rstd_from_ss(sm, D)
                    kb.stt(hn.t[:], h.t[:], sm.t[:, 3:4], gbc.t[:], ALU.mult, ALU.mult, R=[h.b, sm.b, gbc.b], W=[hn.b])
                    for hh in range(2):
                        p = kb.ps()
                        pv = p.t[:].bitcast(BF16)
                        kb.tr([(pv[:, j * 128:(j + 1) * 128], hn.t[:, (hh * 8 + j) * 128:(hh * 8 + j + 1) * 128]) for j in range(8)],
                              self.ident, R=[hn.b], W=[p.b])
                        kb.copy("act" if hh == 0 else "dve", hnT.t[:, hh * 8:(hh + 1) * 8, :], pv.rearrange("p (k t) -> p k t", k=8),
                                R=[p.b], W=[hnT.b])
                        kb.psf(p)
                    kb.dma("sp", [(d["hnT_d"][:, :, tok:tok + 128].rearrange("c p t -> p c t"), hnT.t[:])], R=[hnT.b], W=[])

            mmblk(0)
            for blk in range(4):
                if blk + 1 < 4:
                    mmblk(blk + 1)
                postblk(blk)
            kb.barrier()

    def ffn_stage(self):
        kb, d, W = self.kb, self.d, self.W
        with ExitStack() as es:
            hnTs = [kb.sb(f"f_hnT{i}", [128, 16, 512], BF16, es) for i in range(2)]
            kb.dma("sp", [(hnTs[0].t[:], d["hnT_d"][:, :, 0:512].rearrange("c p t -> p c t"))], W=[hnTs[0].b])
            aT = kb.sb("f_aT", [128, 64, 512], BF16, es)
            hs = [kb.sb(f"f_h{i}", [128, D], F32, es) for i in range(4)]
            rr = [kb.sb(f"f_r{i}", [128, 512], F32, es) for i in range(2)]
            for blk in range(4):
                t0 = blk * 512
                hnT = hnTs[blk % 2]
                n = 0
                for fg in range(16):
                    w = W.acquire([("@c_f1", fg, 8192)])
                    for c4 in range(4):
                        p = kb.ps()
                        kb.mm([(p.t[:, :], w.t[:, kc, c4 * 128:(c4 + 1) * 128], hnT.t[:, kc, :], kc == 0, kc == 15) for kc in range(16)],
                              R=[w.b, hnT.b], W=[p.b])
                        r_ = rr[n % 2]
                        n += 1
                        kb.act(r_.t[:], p.t[:, :], AF.Relu, R=[p.b], W=[r_.b])
                        kb.psf(p)
                        kb.tt(aT.t[:, fg * 4 + c4, :], r_.t[:], r_.t[:], ALU.mult, R=[r_.b], W=[aT.b])
                    W.release(w)
                for tl in range(4):
                    kb.dma("sp", [(hs[tl].t[:], d["h_d"][t0 + tl * 128:t0 + (tl + 1) * 128, :])], W=[hs[tl].b])
                if blk + 1 < 4:
                    nb = hnTs[(blk + 1) % 2]
                    kb.dma("sp", [(nb.t[:], d["hnT_d"][:, :, t0 + 512:t0 + 1024].rearrange("c p t -> p c t"))], W=[nb.b])
                for cb in range(4):
                    pacc = [kb.ps() for _ in range(4)]
                    for kg in range(4):
                        w = W.acquire([("@c_f2", cb * 4 + kg, 8192)])
                        for tl in range(4):
                            kb.mm([(pacc[tl].t[:, :], aT.t[:, kg * 16 + kc, tl * 128:(tl + 1) * 128], w.t[:, kc, :],
                                    kg == 0 and kc == 0, kg == 3 and kc == 15) for kc in range(16)],
                                  R=[w.b, aT.b], W=[pacc[tl].b])
                        W.release(w)
                    for tl in range(4):
                        hv = hs[tl].t[:, cb * 512:(cb + 1) * 512]
                        kb.tt(hv, hv, pacc[tl].t[:, :], ALU.add, R=[pacc[tl].b, hs[tl].b], W=[hs[tl].b])
                    kb.psf(*pacc)
                for tl in range(4):
                    tok = t0 + tl * 128
                    kb.dma("sp", [(d["y"][tok:tok + 128, :], hs[tl].t[:])], R=[hs[tl].b], W=[])
            kb.barrier()


_CACHE = {}


def get_program(dbg=False):
    import os
    STOP[0] = int(os.environ.get("KSTOP", "0"))
    if dbg not in _CACHE:
        b0 = Builder(dbg, None)
        b0.build()
        sched = list(b0.W.rec)
        b1 = Builder(dbg, sched)
        nc = b1.build()
        _CACHE[dbg] = nc
    return _CACHE[dbg]


def core_inputs(c, x_prompt, x_sample, shared, w_in, w_gla_gate, b_gla_gate):
    if c < 4:
        x_own = x_prompt[c]
        x_ctx = np.zeros((T, D), np.float32)
        rev, has_halo = False, False
        pos_own = np.arange(T)
        pos_tail = np.zeros(1024, np.int64)
    else:
        s, half = (c - 4) // 2, (c - 4) % 2
        if half == 1:
            x_own = x_sample[s, T:2 * T]
            x_ctx = x_sample[s, 0:T]
            rev = False
            pos_own = np.arange(T, 2 * T)
            pos_tail = np.arange(T - 1024, T)
        else:
            x_own = x_sample[s, 0:T][::-1]
            x_ctx = x_sample[s, T:2 * T][::-1]
            rev = True
            pos_own = np.arange(T)[::-1]
            pos_tail = np.arange(T, T + 1024)[::-1]
        has_halo = True
    di = 1 if rev else 0
    lrc = (C_LRF, C_LRB)
    w_lr = np.concatenate([w_in[:, lrc[di]:lrc[di] + 16], w_in[:, lrc[1 - di]:lrc[1 - di] + 16]], axis=1)
    w_gate = np.stack([w_gla_gate[di], w_gla_gate[1 - di]], axis=1)
    bg = np.stack([b_gla_gate[di], b_gla_gate[1 - di]], axis=0)
    b_gate = bg.reshape(2, 8, 128).transpose(2, 0, 1).reshape(128, 16)
    pos = np.concatenate([pos_tail, pos_own]).astype(np.float32)
    inv_freq = (np.float32(500000.0) ** (-np.arange(0, 32, 2, dtype=np.float32) / np.float32(32))).astype(np.float32)
    ang = (pos[:, None] * inv_freq[None, :]).astype(np.float32)
    cs = np.cos(ang.astype(np.float64)).astype(np.float32).T
    sn = np.sin(ang.astype(np.float64)).astype(np.float32).T
    cos_t = np.concatenate([cs, cs], axis=0)
    sin_t = np.concatenate([-sn, sn], axis=0)
    am = np.zeros((128, 4, 256), np.float32)
    aidx = np.arange(128)[None, :]
    b = np.arange(64)[:, None]
    A_lo = (b >= aidx).astype(np.float32)
    A_hi = (aidx >= b + 64).astype(np.float32)
    bb = np.arange(128)[:, None]
    Bm = (np.abs(bb - aidx) <= 64).astype(np.float32)
    for v in range(4):
        first = v in (0, 3)
        last = v in (2, 3)
        am[0:64, v, 0:128] = A_lo * (1.0 if (not first or has_halo) else 0.0)
        am[64:128, v, 0:128] = A_hi * (0.0 if last else 1.0)
        am[:, v, 128:256] = Bm
    d = dict(shared)
    d.update(x_own=np.ascontiguousarray(x_own), x_ctx=np.ascontiguousarray(x_ctx), w_lr=np.ascontiguousarray(w_lr),
             w_gate=np.ascontiguousarray(w_gate), b_gate=np.ascontiguousarray(b_gate),
             cos_t=np.ascontiguousarray(cos_t), sin_t=np.ascontiguousarray(sin_t), amask=am)
    return d, rev


def shared_inputs(norm_mix, w_in, q_norm, k_norm, gla_norm, w_branch_attn, w_branch_gla, w_out, norm_ffn, w_ff1, w_ff2):
    perm = np.zeros((32, 32), np.float32)
    for i in range(16):
        perm[16 + i, i] = 1.0
        perm[i, 16 + i] = 1.0
    jj = np.arange(128)[:, None]
    ii = np.arange(128)[None, :]
    trim = np.stack([(jj <= ii).astype(np.float32), (jj > ii).astype(np.float32)], axis=1)
    rmask = np.ones((128, 512), np.float32)
    rmask[:, 0::128] = 0.0
    qk_gain = np.concatenate([q_norm.T, k_norm.T], axis=1)
    return dict(w_in=w_in, norm_mix=norm_mix, norm_ffn=norm_ffn, gla_norm=gla_norm, qk_gain=np.ascontiguousarray(qk_gain),
                w_ba=w_branch_attn, w_bg=w_branch_gla, w_out=w_out, w_ff1=w_ff1, w_ff2=w_ff2,
                ident=np.eye(128, dtype=np.float32), perm=perm, trim=np.ascontiguousarray(trim), rmask=rmask)


def kernel(x_prompt, x_sample, norm_mix, w_in, q_norm, k_norm, w_gla_gate, b_gla_gate, gla_norm,
           w_branch_attn, w_branch_gla, w_out, norm_ffn, w_ff1, w_ff2, _cores=None, _dbg=False):
    f = lambda a: np.ascontiguousarray(np.asarray(a, dtype=np.float32))
    x_prompt, x_sample = f(x_prompt), f(x_sample)
    shared = shared_inputs(f(norm_mix)[0], f(w_in)[0], f(q_norm)[0], f(k_norm)[0], f(gla_norm)[0], f(w_branch_attn)[0],
                           f(w_branch_gla)[0], f(w_out)[0], f(norm_ffn)[0], f(w_ff1)[0], f(w_ff2)[0])
    cores = list(range(NCORES)) if _cores is None else list(_cores)
    in_maps, revs = [], []
    for c in cores:
        m, rev = core_inputs(c, x_prompt, x_sample, shared, shared["w_in"], f(w_gla_gate)[0], f(b_gla_gate)[0])
        in_maps.append(m)
        revs.append(rev)
    nc = get_program(_dbg)
    res = run_bass_kernel_spmd(nc, in_maps, core_ids=list(range(len(cores))))
    if _dbg:
        return res, revs
    y_prompt = np.zeros((4, T, D), np.float32)
    y_sample = np.zeros((2, 2 * T, D), np.float32)
    for c, r, rev in zip(cores, res.results, revs):
        y = np.asarray(r["y"], dtype=np.float32)
        if rev:
            y = y[::-1]
        if c < 4:
            y_prompt[c] = y
        else:
            s, half = (c - 4) // 2, (c - 4) % 2
            y_sample[s, half * T:(half + 1) * T] = y
    return (y_prompt, y_sample)
```
